# Optimizing a Trainium2 kernel written in Bass

```python
import jax, jax.numpy as jnp
from jax import lax
import numpy as np

D_MODEL = 1024
BATCH = 8
SEQ = 4096
DEPTH = 2

PLE_DIM = 256
MIX_WIDTH = D_MODEL
POOL_WIDTH = MIX_WIDTH // 2
POOL_WINDOWS = (2, 4, 8, 16)
N_POOL_GROUPS = len(POOL_WINDOWS)
POOL_GROUP_DIM = POOL_WIDTH // N_POOL_GROUPS
SB_WIDTH = MIX_WIDTH - POOL_WIDTH
SB_HEAD_DIM = 64
SB_HEADS = SB_WIDTH // SB_HEAD_DIM
SB_BLOCK = 128
GDN_HEAD_DIM = 128
GDN_HEADS = MIX_WIDTH // GDN_HEAD_DIM
GDN_CONV = 4
GDN_CHUNK = 64
FFN_DIM = 2816
FFN_CONV = 3
EPS = 1e-6
N_EVEN = (DEPTH + 1) // 2
N_ODD = DEPTH // 2
EVEN_IN = POOL_WIDTH + 3 * SB_WIDTH
ODD_IN = 4 * MIX_WIDTH + 2 * GDN_HEADS

kernel_name = 'hybrid_pool_stickbreak_gdn_convffn_ple'


def rmsnorm(x, gain):
    xf = x.astype(jnp.float32)
    y = xf * lax.rsqrt(jnp.mean(xf * xf, axis=-1, keepdims=True) + EPS)
    return (y * gain.astype(jnp.float32)).astype(x.dtype)


def l2norm(x):
    return x * lax.rsqrt(jnp.sum(x * x, axis=-1, keepdims=True) + EPS)


def causal_dwconv(x, w):
    K = w.shape[0]
    T = x.shape[1]
    xp = jnp.pad(x, ((0, 0), (K - 1, 0), (0, 0)))
    return sum(xp[:, i:i + T] * w[i] for i in range(K))


def pool_mixer(u, pool_w, pool_scale):
    B, T, _ = u.shape
    ug = u.reshape(B, T, N_POOL_GROUPS, POOL_GROUP_DIM).astype(jnp.float32)
    cs = jnp.pad(jnp.cumsum(ug, axis=1), ((0, 0), (1, 0), (0, 0), (0, 0)))
    t = jnp.arange(T)
    win = jnp.array(POOL_WINDOWS, dtype=jnp.int32)
    start = jnp.maximum(t[:, None] + 1 - win[None, :], 0)
    g_idx = jnp.arange(N_POOL_GROUPS)[None, :]
    window_sum = cs[:, 1:] - cs[:, start, g_idx]
    count = (t[:, None] + 1 - start).astype(jnp.float32)
    y = window_sum / count[None, :, :, None] - ug
    y = jnp.einsum('btgc,gcd->btgd', y, pool_w.astype(jnp.float32))
    return (y.reshape(B, T, POOL_WIDTH) * pool_scale.astype(jnp.float32)).astype(u.dtype)


def stick_breaking_attention(q, k, v):
    T = q.shape[2]
    scale = SB_HEAD_DIM ** -0.5
    vf = v.astype(jnp.float32)
    outs = []
    for blk in range(T // SB_BLOCK):
        q0 = blk * SB_BLOCK
        end = q0 + SB_BLOCK
        z = jnp.einsum('bhqd,bhkd->bhqk', q[:, :, q0:end], k[:, :, :end]).astype(jnp.float32) * scale
        q_pos = q0 + jnp.arange(SB_BLOCK)
        k_pos = jnp.arange(end)
        valid = k_pos[None, :] < q_pos[:, None]
        log_1m = jnp.where(valid, jax.nn.log_sigmoid(-z), 0.0)
        log_keep = lax.cumsum(log_1m, axis=3, reverse=True) - log_1m
        a = jnp.where(valid, jnp.exp(jax.nn.log_sigmoid(z) + log_keep), 0.0)
        outs.append(jnp.einsum('bhqk,bhkd->bhqd', a, vf[:, :, :end]))
    return jnp.concatenate(outs, axis=2).astype(v.dtype)


def gated_delta_rule_chunked(q, k, v, g, beta):
    B, H, T, dk = q.shape
    dv = v.shape[-1]
    n = T // GDN_CHUNK
    q = q * dk ** -0.5

    def chunks(a):
        return a.reshape(B, H, n, GDN_CHUNK, *a.shape[3:])

    q, k, v, g, beta = chunks(q), chunks(k), chunks(v), chunks(g), chunks(beta)
    gc = jnp.cumsum(g, axis=-1)
    idx = jnp.arange(GDN_CHUNK)
    incl = idx[:, None] >= idx[None, :]
    strict = idx[:, None] > idx[None, :]
    decay = jnp.where(incl, jnp.exp(jnp.where(incl, gc[..., :, None] - gc[..., None, :], 0.0)), 0.0)
    k_beta = k * beta[..., None]
    a_mat = jnp.where(strict, jnp.einsum('bhncd,bhnsd->bhncs', k_beta, k) * decay, 0.0)
    rhs = jnp.concatenate([v * beta[..., None], k_beta * jnp.exp(gc)[..., None]], axis=-1)
    sol = lax.linalg.triangular_solve(jnp.eye(GDN_CHUNK, dtype=a_mat.dtype) + a_mat, rhs,
                                      left_side=True, lower=True, unit_diagonal=True)
    u, w = sol[..., :dv], sol[..., dv:]
    qk = jnp.einsum('bhncd,bhnsd->bhncs', q, k) * decay
    q_dec = q * jnp.exp(gc)[..., None]
    k_dec = k * jnp.exp(gc[..., -1:] - gc)[..., None]
    g_last = jnp.exp(gc[..., -1])
    xs = (jnp.moveaxis(qk, 2, 0), jnp.moveaxis(u, 2, 0), jnp.moveaxis(w, 2, 0),
          jnp.moveaxis(q_dec, 2, 0), jnp.moveaxis(k_dec, 2, 0), jnp.moveaxis(g_last, 2, 0))

    def step(state, inp):
        qk_c, u_c, w_c, q_c, k_c, gl = inp
        v_new = u_c - jnp.einsum('bhcd,bhde->bhce', w_c, state)
        o = jnp.einsum('bhcd,bhde->bhce', q_c, state) + jnp.einsum('bhcs,bhse->bhce', qk_c, v_new)
        state = state * gl[..., None, None] + jnp.einsum('bhcd,bhce->bhde', k_c, v_new)
        return state, o

    s0 = jnp.zeros((B, H, dk, dv), jnp.float32)
    _, o = lax.scan(step, s0, xs)
    return jnp.moveaxis(o, 0, 2).reshape(B, H, T, dv)


def even_mixer(h, w_in, pool_w, pool_scale, w_out):
    B, T, _ = h.shape
    proj = h @ w_in
    u, q, k, v = jnp.split(proj, [POOL_WIDTH, POOL_WIDTH + SB_WIDTH, POOL_WIDTH + 2 * SB_WIDTH], axis=-1)
    pool_out = pool_mixer(u, pool_w, pool_scale)

    def heads(a):
        return a.reshape(B, T, SB_HEADS, SB_HEAD_DIM).transpose(0, 2, 1, 3)

    attn = stick_breaking_attention(heads(q), heads(k), heads(v))
    attn = attn.transpose(0, 2, 1, 3).reshape(B, T, SB_WIDTH)
    return jnp.concatenate([pool_out, attn], axis=-1) @ w_out


def odd_mixer(h, w_in, conv_w, a_log, dt_bias, norm_w, w_out):
    B, T, _ = h.shape
    proj = h @ w_in
    qkv, z, b, a = jnp.split(proj, [3 * MIX_WIDTH, 4 * MIX_WIDTH, 4 * MIX_WIDTH + GDN_HEADS], axis=-1)
    qkv = jax.nn.silu(causal_dwconv(qkv, conv_w))
    q, k, v = jnp.split(qkv, 3, axis=-1)

    def heads(x_):
        return x_.reshape(B, T, GDN_HEADS, GDN_HEAD_DIM).transpose(0, 2, 1, 3).astype(jnp.float32)

    q, k, v = l2norm(heads(q)), l2norm(heads(k)), heads(v)
    beta = jax.nn.sigmoid(b.astype(jnp.float32)).transpose(0, 2, 1)
    g = (-jnp.exp(a_log.astype(jnp.float32))
         * jax.nn.softplus(a.astype(jnp.float32) + dt_bias.astype(jnp.float32))).transpose(0, 2, 1)
    o = gated_delta_rule_chunked(q, k, v, g, beta).transpose(0, 2, 1, 3)
    o = o * lax.rsqrt(jnp.mean(o * o, axis=-1, keepdims=True) + EPS) * norm_w.astype(jnp.float32)
    o = o * jax.nn.silu(z.reshape(B, T, GDN_HEADS, GDN_HEAD_DIM).astype(jnp.float32))
    return o.reshape(B, T, MIX_WIDTH).astype(h.dtype) @ w_out


def conv_ffn(h, w_up, conv_w, w_down):
    up = causal_dwconv(h @ w_up, conv_w)
    gate, val = jnp.split(up, 2, axis=-1)
    return (jax.nn.silu(gate) * val) @ w_down


def setup_inputs(seed: int = 0) -> dict:
    key = jax.random.key(seed)
    ks = jax.random.split(key, 24)
    f32 = jnp.float32

    def dense(k_, shape, fan_in):
        return jax.random.normal(k_, shape, f32) * fan_in ** -0.5

    def gain(k_, shape):
        return 1.0 + 0.02 * jax.random.normal(k_, shape, f32)

    dt = jnp.exp(jax.random.uniform(ks[11], (N_ODD, GDN_HEADS), f32, np.log(1e-3), np.log(1e-1)))
    return {
        'x': jax.random.normal(ks[0], (BATCH, SEQ, D_MODEL), f32),
        'p': jax.random.normal(ks[1], (DEPTH, BATCH, SEQ, PLE_DIM), f32),
        'mix_norm_e': gain(ks[2], (N_EVEN, D_MODEL)),
        'w_in_e': dense(ks[3], (N_EVEN, D_MODEL, EVEN_IN), D_MODEL),
        'pool_w': dense(ks[4], (N_EVEN, N_POOL_GROUPS, POOL_GROUP_DIM, POOL_GROUP_DIM), POOL_GROUP_DIM),
        'pool_scale': 1.0 + 0.1 * jax.random.normal(ks[5], (N_EVEN, POOL_WIDTH), f32),
        'w_out_e': dense(ks[6], (N_EVEN, MIX_WIDTH, D_MODEL), MIX_WIDTH),
        'mix_norm_o': gain(ks[7], (N_ODD, D_MODEL)),
        'w_in_o': dense(ks[8], (N_ODD, D_MODEL, ODD_IN), D_MODEL),
        'conv_qkv_o': dense(ks[9], (N_ODD, GDN_CONV, 3 * MIX_WIDTH), GDN_CONV),
        'a_log_o': jnp.log(jax.random.uniform(ks[10], (N_ODD, GDN_HEADS), f32, 1.0, 16.0)),
        'dt_bias_o': dt + jnp.log(-jnp.expm1(-dt)),
        'gdn_norm_o': gain(ks[12], (N_ODD, GDN_HEAD_DIM)),
        'w_out_o': dense(ks[13], (N_ODD, MIX_WIDTH, D_MODEL), MIX_WIDTH),
        'ffn_norm': gain(ks[14], (DEPTH, D_MODEL)),
        'w_up': dense(ks[15], (DEPTH, D_MODEL, 2 * FFN_DIM), D_MODEL),
        'ffn_conv': dense(ks[16], (DEPTH, FFN_CONV, 2 * FFN_DIM), FFN_CONV),
        'w_down': dense(ks[17], (DEPTH, FFN_DIM, D_MODEL), FFN_DIM),
        'ple_norm': gain(ks[18], (DEPTH, D_MODEL)),
        'w_ple_gate': dense(ks[19], (DEPTH, D_MODEL, D_MODEL), D_MODEL),
        'w_ple': dense(ks[20], (DEPTH, PLE_DIM, D_MODEL), PLE_DIM),
        'final_norm': gain(ks[21], (D_MODEL,)),
    }


def reference(x, p, mix_norm_e, w_in_e, pool_w, pool_scale, w_out_e,
              mix_norm_o, w_in_o, conv_qkv_o, a_log_o, dt_bias_o, gdn_norm_o, w_out_o,
              ffn_norm, w_up, ffn_conv, w_down, ple_norm, w_ple_gate, w_ple, final_norm):
    for i in range(DEPTH):
        j = i // 2
        if i % 2 == 0:
            x = x + even_mixer(rmsnorm(x, mix_norm_e[j]), w_in_e[j], pool_w[j], pool_scale[j], w_out_e[j])
        else:
            x = x + odd_mixer(rmsnorm(x, mix_norm_o[j]), w_in_o[j], conv_qkv_o[j], a_log_o[j],
                              dt_bias_o[j], gdn_norm_o[j], w_out_o[j])
        x = x + conv_ffn(rmsnorm(x, ffn_norm[i]), w_up[i], ffn_conv[i], w_down[i])
        gate = jax.nn.sigmoid(rmsnorm(x, ple_norm[i]) @ w_ple_gate[i])
        x = x + (p[i] @ w_ple[i]) * gate
    return rmsnorm(x, final_norm)
```

```python
import numpy as np
from contextlib import ExitStack
import concourse.bass as bass
import concourse.mybir as mybir
from concourse.bass_utils import run_bass_kernel_spmd

F32 = mybir.dt.float32
BF16 = mybir.dt.bfloat16
AF = mybir.ActivationFunctionType
ALU = mybir.AluOpType
AX = mybir.AxisListType

D = 1024
SEQ = 4096
NB = 8
FFN = 2816
PLE = 256
EPS = 1e-6
TT = 512
NDS = 8
NCONST = 19
RAW_SKIP_N = 1 << 30
C_ULE, C_UGT, C_MSTRICT, C_MINCLT, C_MOFF = 8, 9, 10, 11, 12


class Tile:
    def __init__(self, name, ap):
        self.name = name
        self.ap = ap
        self.lw = None
        self.rd = {}

    def __getitem__(self, idx):
        return V(self, self.ap[idx])

    @property
    def v(self):
        return V(self, self.ap)


class V:
    def __init__(self, tile, ap):
        self.tile = tile
        self.ap = ap

    def __getitem__(self, idx):
        return V(self.tile, self.ap[idx])


class CT:
    def __init__(self, name, ap, n):
        self.name = name
        self.ap = ap
        self.tiles = [Tile("%s_%d" % (name, c), ap[:, c, :]) for c in range(n)]

    def __getitem__(self, idx):
        p, c, f = idx
        t = self.tiles[c]
        return V(t, t.ap[p, f])

    @property
    def v(self):
        return self


class Op:
    __slots__ = ("eng", "fn", "waits", "inc", "idx", "dtok", "n")

    def __init__(self, eng, fn):
        self.eng = eng
        self.fn = fn
        self.waits = []
        self.inc = False
        self.dtok = None
        self.n = 0


ENGS = ("pe", "act", "dve", "pool", "sp")


class Prog:
    def __init__(self, nc):
        self.nc = nc
        self.ops = {e: [] for e in ENGS}
        self.ndma = {e: 0 for e in ENGS}
        self.out_toks = []
        self.marks = []

    @staticmethod
    def _tiles(lst):
        out = []
        for t in lst:
            if t is None:
                continue
            if isinstance(t, CT):
                out.extend(t.tiles)
            else:
                out.append(t.tile if isinstance(t, V) else t)
        return out

    def op(self, eng, fn, reads=(), writes=(), dma=False, is_out=False, n=0):
        R = self._tiles(reads)
        W = self._tiles(writes)
        o = Op(eng, fn)
        o.n = n
        o.idx = len(self.ops[eng])
        deps = set()
        for t in R:
            if t.lw is not None:
                if t.lw[0] == "d" or t.lw[1] != eng:
                    deps.add(t.lw)
                elif eng != "pe" and self.ops[eng][t.lw[2]].n < RAW_SKIP_N:
                    deps.add(t.lw)
            if t.name.startswith("ps"):
                for tok in t.rd.values():
                    if tok[0] == "e" and tok[1] != eng:
                        deps.add(tok)
        for t in W:
            if t.lw is not None:
                if t.lw[0] == "d" or t.lw[1] != eng or dma:
                    deps.add(t.lw)
            for tok in t.rd.values():
                if tok[0] == "d" or tok[1] != eng or dma:
                    deps.add(tok)
        if dma:
            n = self.ndma[eng]
            self.ndma[eng] = n + 1
            sem = "d_%s_%d" % (eng, n % NDS)
            if n >= NDS:
                deps.add(("d", sem, 16 * (n // NDS)))
            tok = ("d", sem, 16 * (n // NDS + 1))
            o.dtok = tok
            if is_out:
                self.out_toks.append(tok)
        else:
            tok = ("e", eng, o.idx)
        for d in deps:
            if d[0] == "e":
                self.ops[d[1]][d[2]].inc = True
            o.waits.append(d)
        for t in R:
            key = tok[1] if tok[0] == "e" else tok[1:]
            t.rd[key] = tok
        for t in W:
            t.lw = tok
            t.rd = {}
        self.ops[eng].append(o)
        return o

    def mark(self, label):
        self.marks.append((label, {e: len(self.ops[e]) for e in ENGS}))

    def finish(self):
        o = Op("sp", None)
        o.idx = len(self.ops["sp"])
        o.waits = list(self.out_toks)
        self.ops["sp"].append(o)

    def barrier(self, tiles):
        toks = []
        for e in ENGS:
            for last in reversed(self.ops[e]):
                if last.fn is not None and last.dtok is None:
                    last.inc = True
                    toks.append(("e", e, last.idx))
                    break
        for e in ENGS:
            n = self.ndma[e]
            for j in range(max(0, n - NDS), n):
                toks.append(("d", "d_%s_%d" % (e, j % NDS), 16 * (j // NDS + 1)))
        for i, t in enumerate(tiles):
            t.lw = None
            t.rd = {("bar", i): tk for i, tk in enumerate(toks)}

    def alias_sync(self, from_tiles, to_tiles):
        toks = {}
        for f in from_tiles:
            if f.lw is not None:
                toks[("lw", f.lw)] = f.lw
            for k, tk in f.rd.items():
                toks[(k, tk)] = tk
        for t in to_tiles:
            t.lw = None
            t.rd = dict(toks)

    def emit(self):
        nc = self.nc
        with ExitStack() as es:
            sems = {}
            for e in ENGS:
                sems[e] = es.enter_context(nc.semaphore("s_" + e))
                for j in range(min(NDS, self.ndma[e])):
                    k = "d_%s_%d" % (e, j)
                    sems[k] = es.enter_context(nc.semaphore(k))
            cnt = {}
            for e in ENGS:
                c = 0
                arr = []
                for o in self.ops[e]:
                    if o.inc:
                        c += 1
                    arr.append(c)
                cnt[e] = arr
            ops = self.ops

            def run(e, eng):
                seen = {}
                for o in ops[e]:
                    for d in o.waits:
                        if d[0] == "e":
                            k, val = d[1], cnt[d[1]][d[2]]
                        else:
                            k, val = d[1], d[2]
                        if seen.get(k, 0) >= val:
                            continue
                        seen[k] = val
                        eng.wait_ge(sems[k], val)
                    if o.fn is None:
                        continue
                    inst = o.fn(eng)
                    if o.dtok is not None:
                        inst.then_inc(sems[o.dtok[1]], 16)
                    elif o.inc:
                        inst.then_inc(sems[e], 1)

            block = es.enter_context(nc.Block())

            @block.tensor
            def _(eng):
                run("pe", eng)

            @block.scalar
            def _(eng):
                run("act", eng)

            @block.vector
            def _(eng):
                run("dve", eng)

            @block.gpsimd
            def _(eng):
                run("pool", eng)

            @block.sync
            def _(eng):
                run("sp", eng)


class Ctx:
    pass


DBG = 99


def pipeline(items, stages):
    n, S = len(items), len(stages)
    for i in range(n + S - 1):
        for k in range(S):
            j = i - k
            if 0 <= j < n:
                stages[k](items[j])


class Pool:
    def __init__(self, tiles):
        self.tiles = tiles
        self.i = 0

    def get(self):
        t = self.tiles[self.i % len(self.tiles)]
        self.i += 1
        return t


def slab_layout(w, width=256):
    K, N = w.shape
    return np.ascontiguousarray(w.reshape(K // 128, 128, N // width, width).transpose(2, 1, 0, 3))


def vec_layout(v):
    return np.ascontiguousarray(v.reshape(-1, 128).T)


VEC_SPECS = [
    ("mix_norm_e", 8), ("pool_scale", 4), ("mix_norm_o", 8),
    ("ffn_norm0", 8), ("ffn_norm1", 8), ("ple_norm0", 8), ("ple_norm1", 8), ("final_norm", 8),
    ("ffn_conv0", 132), ("ffn_conv1", 132), ("conv_qkv", 96), ("gdn_norm", 1),
]
VEC_OFF = {}
_o = 0
for _n, _c in VEC_SPECS:
    VEC_OFF[_n] = _o
    _o += _c
NV = _o


def build(T=SEQ, do_mix=(True, True), debug=False):
    nc = bass.Bass("TRN2", target_bir_lowering=False)
    P = Prog(nc)
    NT = T // TT
    C = Ctx()
    es = ExitStack()

    def dram_in(name, shape, dt=F32):
        return nc.dram_tensor(name, list(shape), dt, kind="ExternalInput").ap()

    xT_d = dram_in("xT", [D, T])
    pT_d = dram_in("pT", [2, PLE, T])
    vecs_d = dram_in("vecs", [128, NV])
    w_up_d = [dram_in("w_up%d" % i, [2 * FFN // 256, 128, 8, 256]) for i in range(2)]
    w_down_d = [dram_in("w_down%d" % i, [D // 256, 128, 22, 256]) for i in range(2)]
    w_pg_d = [dram_in("w_pg%d" % i, [D // 256, 128, 8, 256]) for i in range(2)]
    w_ple_d = [dram_in("w_ple%d" % i, [D // 256, 128, 2, 256]) for i in range(2)]
    consts_d = dram_in("consts", [128, NCONST, 128])
    w_in_e_d = dram_in("w_in_e", [8, 128, 8, 256])
    w_out_e_d = dram_in("w_out_e", [4, 128, 8, 256])
    pool_w_d = dram_in("pool_w", [128, 4, 128])
    w_in_o_d = dram_in("w_in_o", [16, 128, 8, 256])
    w_out_o_d = dram_in("w_out_o", [4, 128, 8, 256])
    wba_d = dram_in("wba", [128, 8, 16])
    hv_d = dram_in("hv", [128, 16])
    yT_d = nc.dram_tensor("yT", [D, T], F32, kind="ExternalOutput").ap()
    xs_d = nc.dram_tensor("xs", [D, T], F32, kind="Internal").ap()

    def sb(name, shape, dt=F32):
        return es.enter_context(nc.sbuf_tensor("sb_" + name, list(shape), dt))

    def sbt(name, shape, dt=F32):
        t = sb(name, shape, dt)
        return Tile(name, t[:])

    def dt_tile(name, ap):
        return Tile(name, ap)

    vecs = sbt("vecs", [128, NV])
    consts = sbt("consts", [128, NCONST, 128])
    cb = sbt("cb", [128, 8, 128], BF16)
    P.op("sp", lambda e: e.dma_start(out=vecs.ap, in_=vecs_d), writes=[vecs], dma=True)
    P.op("sp", lambda e: e.dma_start(out=consts.ap, in_=consts_d), writes=[consts], dma=True)
    P.op("dve", lambda e: e.tensor_copy(out=cb.ap, in_=consts.ap[:, 0:8, :]), reads=[consts], writes=[cb])
    ones_b = cb[:, 1, :]

    def vcol(name, c):
        o = VEC_OFF[name] + c
        return vecs[:, o:o + 1]

    xT = CT("xT", sb("xT", [128, 8, TT])[:], 8)
    hT = CT("hT", sb("hT", [128, 8, TT], BF16)[:], 8)
    actT_all = sb("actT", [128, 22, TT], BF16)
    actT = [Tile("actT%d" % j, actT_all[:, j, :]) for j in range(22)]
    actT2d = actT_all[:].rearrange("p c t -> p (c t)")
    actTf = actT2d.bitcast(F32)
    w32 = Pool([sbt("w32_%d" % i, [128, TT + 32]) for i in range(10)])
    bigf = sb("big", [128, 16384], F32)[:]
    big = bigf.bitcast(BF16)
    U2f = sb("U2", [128, 5120], F32)[:]
    U2b = U2f.bitcast(BF16)
    uht = [Tile("uh%d" % i, U2f[:, i * 528:(i + 1) * 528]) for i in range(4)]
    qT = Tile("qT", U2b[:, 4224:6272].rearrange("p (c t) -> p c t", c=4))
    Sacc = Pool([Tile("Sacc%d" % i, U2b[:, 6272 + i * 512:6272 + (i + 1) * 512]) for i in range(2)])
    uhalo = sbt("uhalo", [128, 4, 16])
    tmp16 = sbt("tmp16", [128, 16])
    knf = [Tile("knf%d" % h, bigf[:, h * 512:(h + 1) * 512]) for h in range(8)]
    vf = [Tile("vf%d" % h, bigf[:, 4096 + h * 512:4096 + (h + 1) * 512]) for h in range(8)]
    qnb = [Tile("qnb%d" % h, big[:, 16384 + h * 512:16384 + (h + 1) * 512]) for h in range(8)]
    knb = [Tile("knb%d" % h, big[:, 20480 + h * 512:20480 + (h + 1) * 512]) for h in range(8)]
    szT = [Tile("szT%d" % h, big[:, 24576 + h * 512:24576 + (h + 1) * 512]) for h in range(8)]
    Sst = [Tile("Sst%d" % h, bigf[:, 14336 + h * 128:14336 + (h + 1) * 128]) for h in range(8)]
    Sbb = [Tile("Sbb%d" % h, big[:, 30720 + h * 128:30720 + (h + 1) * 128]) for h in range(8)]
    sm = [Tile("sm%d" % j, bigf[:, 15872 + j * 64:15872 + (j + 1) * 64].rearrange("p (k h) -> p k h", k=8))
          for j in range(4)]
    nega = Tile("nega", bigf[:, 15872 + 256:15872 + 264])
    ssq = [Tile("ssq%d" % i, bigf[:, 15872 + 272 + i:15872 + 273 + i]) for i in range(4)]
    gt = {}
    GNAMES = ("InvA4", "InvB4", "InvTA4", "InvTB4", "qkT4", "kbd4", "kdc4", "vb4", "wT4", "vnew4")
    gts = [{}, {}]
    gts[0]["A4"] = Tile("A4_0", U2f[:, 0:512])
    gts[0]["u4"] = Tile("u4_0", U2f[:, 512:1024])
    for n_, nm in enumerate(GNAMES + ("I4", "MS4", "MI4")):
        t_ = Tile(nm + "_0", U2b[:, 2048 + n_ * 512:2048 + (n_ + 1) * 512])
        if nm in GNAMES:
            gts[0][nm] = t_
        else:
            gt[nm] = t_
    gts[1]["A4"] = Tile("A4_1", actTf[:, 8 * 256:8 * 256 + 512])
    gts[1]["u4"] = Tile("u4_1", actTf[:, 10 * 256:10 * 256 + 512])
    for n_, nm in enumerate(GNAMES):
        gts[1][nm] = Tile(nm + "_1", actT2d[:, (12 + n_) * 512:(13 + n_) * 512])
    ssq4 = [Tile("ssq4_%d" % i, bigf[:, 15872 + 280 + 4 * i:15872 + 284 + 4 * i]) for i in range(2)]
    _qh = sb("qhalo", [128, 24, 3])
    qhalo = [Tile("qhalo%d" % i, _qh[:, i, :]) for i in range(24)]
    wba = sbt("wba", [128, 8, 16], BF16)
    hv = sbt("hv", [128, 16])
    l1_tiles = knf + vf + qnb + knb + szT + Sst + Sbb + sm + [nega] + ssq4 + ssq + list(gt.values()) + list(gts[0].values())
    poolw = sbt("poolw", [128, 4, 128], BF16)
    w16 = Pool([sbt("w16_%d" % i, [128, TT], BF16) for i in range(8)])
    wsl = Pool([sbt("wsl%d" % i, [128, 8, 256], BF16) for i in range(7)])
    _ps = [Tile("ps%d" % i, es.enter_context(nc.psum_tensor("psum%d" % i, [128, 512], F32))[:])
           for i in range(8)]
    psum = Pool(_ps[0:6])
    psO = Pool(_ps[6:8])
    _fh = sb("halo", [128, 44, 2])
    halo = [Tile("halo%d" % i, _fh[:, i, :]) for i in range(44)]
    pTb = sbt("pTb", [128, 2, TT], BF16)

    dq = ["sp", "pool"]
    C.dqi = 0

    scratch = {}
    C.cast_i = 0

    def load_slab(wd, idx, nk, first, k0=0):
        key = id(wd)
        if key not in scratch:
            G, _, NK, _ = wd.shape
            sc = nc.dram_tensor("wb%d" % len(scratch), [G, 128, NK, 256], BF16, kind="Internal").ap()
            scratch[key] = (sc, {})
        sc, tiles = scratch[key]
        tk = (idx, k0)
        sl = wsl.get()
        if first:
            src = wd[idx][:, k0:k0 + nk, :]
            P.op("pool", lambda e: e.dma_start(out=sl.ap[:, 0:nk, :], in_=src), writes=[sl], dma=True)
            dst = sc[idx][:, k0:k0 + nk, :]
            tiles[tk] = Tile("wb_%d_%d_%d" % (len(scratch), idx, k0), dst)
            P.op("sp", lambda e: e.dma_start(out=dst, in_=sl.ap[:, 0:nk, :]), reads=[sl], writes=[tiles[tk]],
                 dma=True)
        else:
            t_ = tiles[tk]
            P.op("sp", lambda e: e.dma_start(out=sl.ap[:, 0:nk, :], in_=t_.ap), reads=[t_], writes=[sl], dma=True)
        return sl

    def fsz(v):
        n = 1
        for d in v.ap.shape[1:]:
            n *= int(d)
        return n

    def mm(out, lhsT, rhs, start, stop, skip=False):
        P.op("pe", lambda e: e.matmul(out.ap, lhsT=lhsT.ap, rhs=rhs.ap, start=start, stop=stop,
                                      skip_group_check=skip),
             reads=[lhsT, rhs], writes=[out])

    def act(out, in_, func, reads=(), **kw):
        kw2 = {k: (v.ap if isinstance(v, V) else v) for k, v in kw.items()}
        extra = [v for v in kw.values() if isinstance(v, V)]
        P.op("act", lambda e: e.activation(out=out.ap, in_=in_.ap, func=func, **kw2),
             reads=[in_] + extra + list(reads), writes=[out], n=fsz(out))

    def tt(eng, out, in0, in1, op):
        P.op(eng, lambda e: e.tensor_tensor(out=out.ap, in0=in0.ap, in1=in1.ap, op=op),
             reads=[in0, in1], writes=[out], n=fsz(out))

    def ts(eng, out, in0, s1, s2, op0, op1=None):
        a1 = s1.ap if isinstance(s1, V) else s1
        a2 = s2.ap if isinstance(s2, V) else s2
        rd = [in0] + [s for s in (s1, s2) if isinstance(s, V)]
        if op1 is None:
            P.op(eng, lambda e: e.tensor_scalar(out=out.ap, in0=in0.ap, scalar1=a1, scalar2=None, op0=op0),
                 reads=rd, writes=[out], n=fsz(out))
        else:
            P.op(eng, lambda e: e.tensor_scalar(out=out.ap, in0=in0.ap, scalar1=a1, scalar2=a2, op0=op0, op1=op1),
                 reads=rd, writes=[out], n=fsz(out))

    def stt(out, in0, scalar, in1, op0, op1):
        a = scalar.ap if isinstance(scalar, V) else scalar
        rd = [in0, in1] + ([scalar] if isinstance(scalar, V) else [])
        P.op("dve", lambda e: e.scalar_tensor_tensor(out=out.ap, in0=in0.ap, scalar=a, in1=in1.ap, op0=op0, op1=op1),
             reads=rd, writes=[out], n=fsz(out))

    def copy(eng, out, in_):
        P.op(eng, lambda e: e.tensor_copy(out=out.ap, in_=in_.ap), reads=[in_], writes=[out], n=fsz(out))

    def rmsnorm(gname, out_tile=None, out_f32=None):
        ps = psum.get()
        for kc in range(8):
            act(hT[:, kc, :], xT[:, kc, :], AF.Square)
            mm(ps.v, ones_b, hT[:, kc, :], kc == 0, kc == 7)
        rs = w32.get()
        act(rs[:, 0:TT], ps.v, AF.Ln, scale=1.0 / D, bias=EPS)
        act(rs[:, 0:TT], rs[:, 0:TT], AF.Exp, scale=-0.5)
        for kc in range(8):
            if out_f32 is not None:
                stt(out_f32(kc), xT[:, kc, :], vcol(gname, kc), rs[:, 0:TT], ALU.mult, ALU.mult)
            else:
                stt(hT[:, kc, :], xT[:, kc, :], vcol(gname, kc), rs[:, 0:TT], ALU.mult, ALU.mult)

    def ffn_ple(li, ti):
        P.mark("L%d.T%d.ffn_up" % (li, ti))
        for kc in range(2):
            srcp = pT_d[li][kc * 128:(kc + 1) * 128, ti * TT:(ti + 1) * TT]
            P.op("pool", lambda e, kc=kc, srcp=srcp: e.dma_start(out=pTb.ap[:, kc, :], in_=srcp), writes=[pTb],
                 dma=True)
        rmsnorm("ffn_norm%d" % li)
        cw = "ffn_conv%d" % li

        def conv_chunk(f, ps):
            raw = w32.get()
            cv = w32.get()
            if ti == 0:
                P.op("pool", lambda e: e.memset(raw.ap[:, 0:2], 0.0), writes=[raw])
            else:
                copy("pool", raw[:, 0:2], halo[f].v)
            act(raw[:, 2:2 + TT], ps.v, AF.Copy)
            act(cv[:, 0:TT], ps.v, AF.Copy, scale=vcol(cw, f * 3 + 2))
            stt(cv[:, 0:TT], raw[:, 1:1 + TT], vcol(cw, f * 3 + 1), cv[:, 0:TT], ALU.mult, ALU.add)
            stt(cv[:, 0:TT], raw[:, 0:TT], vcol(cw, f * 3 + 0), cv[:, 0:TT], ALU.mult, ALU.add)
            copy("pool", halo[f].v, raw[:, TT:TT + 2])
            return cv

        fslabs = {}

        def fS0(it):
            j = it["j"]
            sg, oc = j // 2, j % 2
            if oc == 0:
                fslabs[sg] = [load_slab(w_up_d[li], sg, 8, ti == 0), load_slab(w_up_d[li], FFN // 256 + sg, 8, ti == 0)]
            it["pss"] = []
            for s_i in range(2):
                ps = psum.get()
                for kc in range(8):
                    mm(ps.v, fslabs[sg][s_i][:, kc, oc * 128:(oc + 1) * 128], hT[:, kc, :], kc == 0, kc == 7)
                it["pss"].append(ps)

        def fS1(it):
            j = it["j"]
            it["cg"] = conv_chunk(j, it["pss"][0])
            it["cvv"] = conv_chunk(22 + j, it["pss"][1])

        def fS2(it):
            j, cg, cvv = it["j"], it["cg"], it["cvv"]
            act(cg[:, 0:TT], cg[:, 0:TT], AF.Silu)
            tt("dve", actT[j].v, cg[:, 0:TT], cvv[:, 0:TT], ALU.mult)

        pipeline([{"j": j} for j in range(22)], [fS0, fS1, fS2])
        P.mark("L%d.T%d.ffn_down" % (li, ti))
        for cg in range(D // 256):
            pss = [psum.get(), psum.get()]
            for kg in range(0, 22, 8):
                nk = min(8, 22 - kg)
                slab = load_slab(w_down_d[li], cg, nk, ti == 0, k0=kg)
                for oc in range(2):
                    for k in range(nk):
                        mm(pss[oc].v, slab[:, k, oc * 128:(oc + 1) * 128], actT[kg + k].v,
                           kg + k == 0, kg + k == 21)
            for oc in range(2):
                c = cg * 2 + oc
                tt("dve", xT[:, c, :], pss[oc].v, xT[:, c, :], ALU.add)
        P.mark("L%d.T%d.ple" % (li, ti))
        rmsnorm("ple_norm%d" % li)
        pslabs = {}

        def qS0(it):
            c = it["c"]
            cg, oc = c // 2, c % 2
            if oc == 0:
                pslabs[cg] = (load_slab(w_pg_d[li], cg, 8, ti == 0), load_slab(w_ple_d[li], cg, 2, ti == 0))
            sg_, sp_ = pslabs[cg]
            pg = psum.get()
            pp = psum.get()
            it["pg"], it["pp"] = pg, pp
            for kc in range(8):
                mm(pg.v, sg_[:, kc, oc * 128:(oc + 1) * 128], hT[:, kc, :], kc == 0, kc == 7)
            for kc in range(2):
                mm(pp.v, sp_[:, kc, oc * 128:(oc + 1) * 128], pTb[:, kc, :], kc == 0, kc == 1)

        def qS1(it):
            g = w32.get()
            it["g"] = g
            act(g[:, 0:TT], it["pg"].v, AF.Sigmoid)

        def qS2(it):
            c, g = it["c"], it["g"]
            tt("dve", g[:, 0:TT], it["pp"].v, g[:, 0:TT], ALU.mult)
            tt("dve", xT[:, c, :], g[:, 0:TT], xT[:, c, :], ALU.add)

        pipeline([{"c": c} for c in range(8)], [qS0, qS1, qS2])

    kT = Tile("kT", big[:, 0:4 * T].rearrange("p (c t) -> p c t", c=4))
    Vc = Tile("Vc", big[:, 4 * T:4 * T + (T // 128) * 512].rearrange("p (b f) -> p b f", f=512))
    C.mix0_init = False

    def mixer0(ti):
        t0 = ti * TT
        if not C.mix0_init:
            C.mix0_init = True
            pwf = w32.get()
            P.op("sp", lambda e: e.dma_start(out=pwf.ap[:, 0:512].rearrange("p (g d) -> p g d", g=4), in_=pool_w_d),
                 writes=[pwf], dma=True)
            copy("dve", poolw.v, V(pwf, pwf.ap[:, 0:512].rearrange("p (g d) -> p g d", g=4)))
        P.mark("L0.T%d.m0_proj" % ti)
        rmsnorm("mix_norm_e")
        uh = []
        for sgi in range(8):
            slab = load_slab(w_in_e_d, sgi, 8, ti == 0)
            if sgi < 6:
                for oc in range(2):
                    ch = sgi * 2 + oc
                    ps = psum.get()
                    for kc in range(8):
                        mm(ps.v, slab[:, kc, oc * 128:(oc + 1) * 128], hT[:, kc, :], kc == 0, kc == 7)
                    if ch < 4:
                        u = uht[ch]
                        uh.append(u)
                        if ti == 0:
                            P.op("pool", lambda e, u=u: e.memset(u.ap[:, 0:16], 0.0), writes=[u])
                        else:
                            copy("pool", u[:, 0:16], uhalo[:, ch, :])
                        act(u[:, 16:16 + TT], ps.v, AF.Copy)
                        copy("pool", uhalo[:, ch, :], u[:, TT:TT + 16])
                    elif ch < 8:
                        act(qT[:, ch - 4, :], ps.v, AF.Copy, scale=0.125)
                    else:
                        copy("dve", kT[:, ch - 8, t0:t0 + TT], ps.v)
            else:
                half = sgi - 6
                for j in range(4):
                    ps = psum.get()
                    for kc in range(8):
                        mm(ps[:, 0:256], hT[:, kc, j * 128:(j + 1) * 128], slab[:, kc, :], kc == 0, kc == 7)
                    copy("dve", Vc[:, ti * 4 + j, half * 256:(half + 1) * 256], ps[:, 0:256])
        P.mark("L0.T%d.m0_pool" % ti)
        for g in range(4):
            w = 2 << g
            u = uh[g]
            s_prev = u
            lo = 0
            for l in range(g + 1):
                sh = 1 << l
                lo += sh
                s_new = w32.get()
                tt("dve", s_new[:, lo:16 + TT], s_prev[:, lo:16 + TT], s_prev[:, lo - sh:16 + TT - sh], ALU.add)
                s_prev = s_new
            y = w16.get()
            stt(y.v, s_prev[:, 16:16 + TT], 1.0 / w, u[:, 16:16 + TT], ALU.mult, ALU.subtract)
            if ti == 0:
                tmp = tmp16
                tt("dve", tmp[:, 0:16], s_prev[:, 16:32], consts[:, 6, g * 16:(g + 1) * 16], ALU.mult)
                tt("dve", y[:, 0:16], tmp[:, 0:16], u[:, 16:32], ALU.subtract)
            ps = psum.get()
            mm(ps.v, poolw[:, g, :], y.v, True, True)
            act(actT[g].v, ps.v, AF.Copy, scale=vcol("pool_scale", g))
        P.mark("L0.T%d.m0_attn" % ti)
        nkb = (t0 + TT) // 128
        steps = []
        for c in range(4):
            for kb in range(nkb - 1, -1, -1):
                steps.append({"c": c, "kb": kb, "first": kb == nkb - 1})
        cur = {}

        def stA(st):
            c, kb = st["c"], st["kb"]
            if st["first"]:
                cur["ops"] = psO.get()
                mm(cur["ops"].v, cb[:, 5, :], qT[:, c, :], True, True)
                cur["S"] = [Sacc.get(), Sacc.get()]
            st["ops"] = cur["ops"]
            st["S"] = cur["S"]
            r = kb - t0 // 128
            q_lo = max(r, 0) * 128
            st["r"], st["q_lo"] = r, q_lo
            st["zp"] = [psum.get(), psum.get()]
            for hh in range(2):
                r0 = hh * 64
                mm(st["zp"][hh][:, q_lo:TT], kT[r0:r0 + 64, c, kb * 128:(kb + 1) * 128], qT[r0:r0 + 64, c, q_lo:TT],
                   True, True)
            st["Lp"] = []
            for hh in range(2):
                E = w32.get()
                act(E[:, q_lo:TT], st["zp"][hh][:, q_lo:TT], AF.Exp)
                Lp = w16.get()
                st["Lp"].append(Lp)
                act(Lp[:, q_lo:TT], E[:, q_lo:TT], AF.Ln, bias=1.0)
                if r >= 0:
                    tt("pool", Lp[:, q_lo:q_lo + 128], Lp[:, q_lo:q_lo + 128], cb[:, 3, :], ALU.mult)

        def stB(st):
            q_lo, r, kb, first = st["q_lo"], st["r"], st["kb"], st["first"]
            for hh in range(2):
                zp, Lp, S = st["zp"][hh], st["Lp"][hh], st["S"][hh]
                mm(zp[:, q_lo:TT], cb[:, 2, :], Lp[:, q_lo:TT], False, first, skip=True)
                if not first:
                    mm(zp[:, q_lo:TT], cb[:, 4, :], S[:, q_lo:TT], False, True, skip=True)
            st["A"] = []
            for hh in range(2):
                zp, Lp, S = st["zp"][hh], st["Lp"][hh], st["S"][hh]
                A = w16.get()
                st["A"].append(A)
                act(A[:, q_lo:TT], zp[:, q_lo:TT], AF.Exp)
                if r >= 0:
                    tt("pool", A[:, q_lo:q_lo + 128], A[:, q_lo:q_lo + 128], cb[:, 3, :], ALU.mult)
                if kb > 0:
                    if first:
                        if q_lo > 0:
                            P.op("pool", lambda e, S=S, q_lo=q_lo: e.memset(S.ap[:, 0:q_lo], 0.0), writes=[S])
                        copy("pool", S[:, q_lo:TT], Lp[:, q_lo:TT])
                    else:
                        tt("pool", S[:, q_lo:TT], S[:, q_lo:TT], Lp[:, q_lo:TT], ALU.add)

        def stC(st):
            c, kb, q_lo, ops_ = st["c"], st["kb"], st["q_lo"], st["ops"]
            for hh in range(2):
                r0 = hh * 64
                A = st["A"][hh]
                last = True
                P.op("pe", lambda e, A=A, r0=r0, last=last: e.matmul(
                    ops_.ap[r0:r0 + 64, q_lo:TT], lhsT=Vc.ap[:, kb, c * 128 + r0:c * 128 + r0 + 64],
                    rhs=A.ap[:, q_lo:TT], start=False, stop=last, tile_position=(0, r0), skip_group_check=True),
                    reads=[Vc, A], writes=[ops_])
            if kb == 0:
                copy("dve", actT[4 + c].v, ops_.v)

        pipeline(steps, [stA, stB, stC])
        P.mark("L0.T%d.m0_out" % ti)
        for cg in range(4):
            slab = load_slab(w_out_e_d, cg, 8, ti == 0)
            for oc in range(2):
                ch = cg * 2 + oc
                ps = psum.get()
                for kc in range(8):
                    mm(ps.v, slab[:, kc, oc * 128:(oc + 1) * 128], actT[kc].v, kc == 0, kc == 7)
                tt("dve", xT[:, ch, :], ps.v, xT[:, ch, :], ALU.add)

    C.mix1_init = False
    I_f = consts[:, 0, :]
    ones_f = consts[:, 1, :]

    def tr(out, in_):
        P.op("pe", lambda e: e.transpose(out=out.ap, in_=in_.ap, identity=I_f.ap), reads=[in_, I_f], writes=[out])

    def mixer1(ti):
        if not C.mix1_init:
            C.mix1_init = True
            P.barrier(l1_tiles)
            wf = w32.get()
            P.op("sp", lambda e: e.dma_start(out=wf.ap[:, 0:128].rearrange("p (k c) -> p k c", k=8), in_=wba_d),
                 writes=[wf], dma=True)
            copy("dve", wba.v, V(wf, wf.ap[:, 0:128].rearrange("p (k c) -> p k c", k=8)))
            P.op("sp", lambda e: e.dma_start(out=hv.ap, in_=hv_d), writes=[hv], dma=True)
            act(nega.v, hv[:, 8:16], AF.Exp)
            ts("dve", nega.v, nega.v, -1.0, None, ALU.mult)
            for hi in range(4):
                copy("pool", V(gt["I4"], gt["I4"].ap[:, hi * 128:(hi + 1) * 128]), cb[:, 0, :])
                copy("pool", V(gt["MS4"], gt["MS4"].ap[:, hi * 128:(hi + 1) * 128]), consts[:, C_MSTRICT, :])
                copy("pool", V(gt["MI4"], gt["MI4"].ap[:, hi * 128:(hi + 1) * 128]), consts[:, C_MINCLT, :])
            for h in range(8):
                P.op("pool", lambda e, h=h: e.memset(Sst[h].ap, 0.0), writes=[Sst[h]])
                P.op("pool", lambda e, h=h: e.memset(Sbb[h].ap, 0.0), writes=[Sbb[h]])
        P.mark("L1.T%d.m1_proj" % ti)
        rmsnorm("mix_norm_o")
        items = [{"ch": ch} for ch in range(32)]
        slabs = {}

        def pS0(it):
            ch = it["ch"]
            sgi, oc = ch // 2, ch % 2
            if oc == 0:
                slabs[sgi] = load_slab(w_in_o_d, sgi, 8, ti == 0)
            slab = slabs[sgi]
            ps = psum.get()
            it["ps"] = ps
            for kc in range(8):
                mm(ps.v, slab[:, kc, oc * 128:(oc + 1) * 128], hT[:, kc, :], kc == 0, kc == 7)

        def pS1(it):
            ch, ps = it["ch"], it["ps"]
            if ch >= 24:
                act(szT[ch - 24].v, ps.v, AF.Silu)
                return
            raw = w32.get()
            cv = w32.get()
            it["cv"] = cv
            if ti == 0:
                P.op("pool", lambda e, raw=raw: e.memset(raw.ap[:, 0:3], 0.0), writes=[raw])
            else:
                copy("pool", raw[:, 0:3], qhalo[ch].v)
            act(cv[:, 0:TT], ps.v, AF.Copy, scale=vcol("conv_qkv", ch * 4 + 3))
            copy("dve", raw[:, 3:3 + TT], ps.v)
            for i in (2, 1, 0):
                stt(cv[:, 0:TT], raw[:, i:i + TT], vcol("conv_qkv", ch * 4 + i), cv[:, 0:TT], ALU.mult, ALU.add)
            copy("pool", qhalo[ch].v, raw[:, TT:TT + 3])

        def pS2(it):
            ch = it["ch"]
            if ch >= 24:
                return
            cv = it["cv"]
            if ch >= 16:
                act(vf[ch - 16].v, cv[:, 0:TT], AF.Silu)
                return
            act(cv[:, 0:TT], cv[:, 0:TT], AF.Silu)
            sqb = w16.get()
            tt("pool", sqb.v, cv[:, 0:TT], cv[:, 0:TT], ALU.mult)
            ps2 = psum.get()
            it["ps2"] = ps2
            mm(ps2.v, ones_b, sqb.v, True, True)

        def pS3(it):
            ch = it["ch"]
            if ch >= 16:
                return
            cv, ps2 = it["cv"], it["ps2"]
            rn = w32.get()
            act(rn[:, 0:TT], ps2.v, AF.Ln, bias=EPS)
            act(rn[:, 0:TT], rn[:, 0:TT], AF.Exp, scale=-0.5)
            if ch < 8:
                stt(qnb[ch].v, cv[:, 0:TT], float(128 ** -0.5), rn[:, 0:TT], ALU.mult, ALU.mult)
            else:
                tt("dve", knf[ch - 8].v, cv[:, 0:TT], rn[:, 0:TT], ALU.mult)
                copy("pool", knb[ch - 8].v, knf[ch - 8].v)

        pipeline(items, [pS0, pS1, pS2, pS3])
        P.mark("L1.T%d.m1_gates" % ti)
        if DBG < 2:
            return
        for j in range(4):
            s_ = sm[j]
            ps = psum.get()
            for kc in range(8):
                mm(ps[:, 0:16], hT[:, kc, j * 128:(j + 1) * 128], wba[:, kc, :], kc == 0, kc == 7)
            act(s_[:, 0, :], ps[:, 0:8], AF.Sigmoid)
            tt("dve", s_[:, 7, :], ps[:, 8:16], hv[:, 0:8], ALU.add)
            act(s_[:, 7, :], s_[:, 7, :], AF.Exp)
            act(s_[:, 7, :], s_[:, 7, :], AF.Ln, bias=1.0)
            tt("dve", s_[:, 1, :], s_[:, 7, :], nega.v, ALU.mult)
            ps2 = psum.get()
            mm(ps2[:, 0:8], consts[:, C_ULE, :], s_[:, 1, :], True, True)
            mm(ps2[:, 8:16], ones_f, s_[:, 1, :], True, True)
            act(s_[:, 2, :], ps2[:, 0:8], AF.Copy)
            act(s_[:, 3, :], ps2[:, 0:8], AF.Exp)
            act(s_[:, 4, :], ps2[:, 8:16], AF.Exp)
            tt("dve", s_[:, 7, :], ps2[:, 8:16], s_[:, 2, :], ALU.subtract)
            act(s_[:, 5, :], s_[:, 7, :], AF.Exp)
            tt("dve", s_[:, 6, :], s_[:, 0, :], s_[:, 3, :], ALU.mult)
        P.mark("L1.T%d.m1_chunks" % ti)
        if DBG < 3:
            return
        def hs_(x, hi):
            return x[:, hi * 128:(hi + 1) * 128]

        P.alias_sync(actT[8:22], list(gts[1].values()))
        for j in range(4):
            s_ = sm[j]
            cs = slice(j * 128, (j + 1) * 128)
            Xs = []
            for hg in range(2):
                Xs.append({"hg": hg, "g": gts[hg], "hs": [(hi, hg * 4 + hi) for hi in range(4)]})

            def gv(X, name, hi):
                t_ = X["g"][name]
                return V(t_, t_.ap[:, hi * 128:(hi + 1) * 128])

            def st1(X):
                X["gU4"], X["gL4"] = w32.get(), w32.get()
                for hi, h in X["hs"]:
                    ts("dve", hs_(X["gU4"], hi), consts[:, C_UGT, :], s_[:, 1, h:h + 1], None, ALU.mult)
                    ts("dve", hs_(X["gL4"], hi), consts[:, C_ULE, :], s_[:, 1, h:h + 1], None, ALU.mult)

            def st2(X):
                X["psD1"], X["psD2"] = psum.get(), psum.get()
                for hi, h in X["hs"]:
                    mm(hs_(X["psD1"], hi), consts[:, C_ULE, :], hs_(X["gU4"], hi), True, True)
                for hi, h in X["hs"]:
                    mm(hs_(X["psD2"], hi), consts[:, C_UGT, :], hs_(X["gL4"], hi), True, True)

            def st3(X):
                X["decS4"], X["decTI4"] = w32.get(), w32.get()
                act(X["decS4"][:, 0:512], X["psD1"].v, AF.Exp)
                tt("pool", X["decS4"][:, 0:512], X["decS4"][:, 0:512], gt["MS4"].v, ALU.mult)
                act(X["decTI4"][:, 0:512], X["psD2"].v, AF.Exp)
                tt("pool", X["decTI4"][:, 0:512], X["decTI4"][:, 0:512], gt["MI4"].v, ALU.mult)

            def st4(X):
                X["psK1"], X["psK2"] = psum.get(), psum.get()
                for hi, h in X["hs"]:
                    mm(hs_(X["psK1"], hi), knb[h][:, cs], knb[h][:, cs], True, True)
                for hi, h in X["hs"]:
                    mm(hs_(X["psK2"], hi), knb[h][:, cs], qnb[h][:, cs], True, True)

            def st5(X):
                for hi, h in X["hs"]:
                    stt(gv(X, "A4", hi), hs_(X["psK1"], hi), s_[:, 0, h:h + 1], hs_(X["decS4"], hi), ALU.mult, ALU.mult)
                tt("dve", X["g"]["qkT4"].v, X["psK2"].v, X["decTI4"][:, 0:512], ALU.mult)

            def st6(X):
                X["psT1"], X["psT2"] = psum.get(), psum.get()
                for hi, h in X["hs"]:
                    tr(hs_(X["psT1"], hi), knf[h][:, cs])
                for hi, h in X["hs"]:
                    tr(hs_(X["psT2"], hi), vf[h][:, cs])

            def st7(X):
                for hi, h in X["hs"]:
                    ts("dve", gv(X, "kbd4", hi), hs_(X["psT1"], hi), s_[:, 6, h:h + 1], None, ALU.mult)
                    ts("dve", gv(X, "kdc4", hi), hs_(X["psT1"], hi), s_[:, 5, h:h + 1], None, ALU.mult)
                    act(gv(X, "vb4", hi), hs_(X["psT2"], hi), AF.Copy, scale=s_[:, 0, h:h + 1])
                X["Inv"], X["InvT"] = gt["I4"], gt["I4"]

            def lv_a(X, l):
                X["Aoff4"] = w16.get()
                for hi, h in X["hs"]:
                    tt("pool", hs_(X["Aoff4"], hi), gv(X, "A4", hi), consts[:, C_MOFF + l, :], ALU.mult)

            def lv_b(X, l):
                X["psY"] = psum.get()
                for hi, h in X["hs"]:
                    mm(hs_(X["psY"], hi), hs_(X["Aoff4"], hi), hs_(X["InvT"], hi), True, True)

            def lv_c(X, l):
                X["ImY4"] = w16.get()
                tt("dve", X["ImY4"].v, gt["I4"].v, X["psY"].v, ALU.subtract)

            def lv_d(X, l):
                if l < 6:
                    X["psI1"] = psum.get()
                    for hi, h in X["hs"]:
                        mm(hs_(X["psI1"], hi), hs_(X["ImY4"], hi), hs_(X["Inv"], hi), True, True)
                X["psI2"] = psum.get()
                for hi, h in X["hs"]:
                    mm(hs_(X["psI2"], hi), hs_(X["Inv"], hi), hs_(X["ImY4"], hi), True, True)

            def lv_e(X, l):
                nT = X["g"]["InvTA4" if l % 2 == 0 else "InvTB4"]
                if l < 6:
                    nI = X["g"]["InvA4" if l % 2 == 0 else "InvB4"]
                    act(nI.v, X["psI1"].v, AF.Copy)
                else:
                    nI = None
                if l == 6 or X["hg"] == 0:
                    act(nT.v, X["psI2"].v, AF.Copy)
                else:
                    copy("dve", nT.v, X["psI2"].v)
                X["Inv"], X["InvT"] = nI, nT

            def st8(X):
                X["psU1"], X["psU2"] = psum.get(), psum.get()
                IT = X["InvT"]
                for hi, h in X["hs"]:
                    mm(hs_(X["psU1"], hi), hs_(IT, hi), gv(X, "vb4", hi), True, True)
                for hi, h in X["hs"]:
                    mm(hs_(X["psU2"], hi), gv(X, "kbd4", hi), hs_(IT, hi), True, True)

            def st9(X):
                act(X["g"]["u4"].v, X["psU1"].v, AF.Copy)
                copy("dve", X["g"]["wT4"].v, X["psU2"].v)

            def st10(X):
                X["psP1"], X["psP2"] = psum.get(), psum.get()
                for hi, h in X["hs"]:
                    mm(hs_(X["psP1"], hi), gv(X, "wT4", hi), Sbb[h].v, True, True)
                for hi, h in X["hs"]:
                    mm(hs_(X["psP2"], hi), qnb[h][:, cs], Sbb[h].v, True, True)

            def st11(X):
                tt("dve", X["g"]["vnew4"].v, X["g"]["u4"].v, X["psP1"].v, ALU.subtract)
                X["o1"], X["o"] = w32.get(), w32.get()
                for hi, h in X["hs"]:
                    act(hs_(X["o1"], hi), hs_(X["psP2"], hi), AF.Copy, scale=s_[:, 3, h:h + 1])

            def st12(X):
                X["psQ1"], X["psQ2"] = psum.get(), psum.get()
                for hi, h in X["hs"]:
                    mm(hs_(X["psQ1"], hi), gv(X, "qkT4", hi), gv(X, "vnew4", hi), True, True)
                for hi, h in X["hs"]:
                    mm(hs_(X["psQ2"], hi), gv(X, "kdc4", hi), gv(X, "vnew4", hi), True, True)

            def st13(X):
                o1, o = X["o1"], X["o"]
                tt("dve", o[:, 0:512], o1[:, 0:512], X["psQ1"].v, ALU.add)
                for hi, h in X["hs"]:
                    stt(Sst[h].v, Sst[h].v, s_[:, 4, h:h + 1], hs_(X["psQ2"], hi), ALU.mult, ALU.add)
                    copy("pool", Sbb[h].v, Sst[h].v)
                tt("pool", o1[:, 0:512], o[:, 0:512], o[:, 0:512], ALU.mult)
                sq_ = ssq4[X["hg"]]
                P.op("dve", lambda e, o1=o1, sq_=sq_: e.tensor_reduce(
                    out=sq_.ap, in_=o1.ap[:, 0:512].rearrange("p (h e) -> p h e", h=4), axis=AX.X, op=ALU.add),
                    reads=[o1], writes=[sq_])
                act(sq_.v, sq_.v, AF.Ln, scale=1.0 / 128, bias=EPS)
                act(sq_.v, sq_.v, AF.Exp, scale=-0.5)

            def st14(X):
                o = X["o"]
                sq_ = ssq4[X["hg"]]
                for hi, h in X["hs"]:
                    ts("dve", hs_(o, hi), hs_(o, hi), sq_[:, hi:hi + 1], None, ALU.mult)
                X["psT3"] = psum.get()
                for hi, h in X["hs"]:
                    tr(hs_(X["psT3"], hi), hs_(o, hi))

            def st15(X):
                for hi, h in X["hs"]:
                    stt(actT[h][:, cs], hs_(X["psT3"], hi), vcol("gdn_norm", 0), szT[h][:, cs], ALU.mult, ALU.mult)

            for st in (st1, st2, st3, st4, st5, st6, st7):
                for X in Xs:
                    st(X)
            for l in range(7):
                for lv in (lv_a, lv_b, lv_c, lv_d, lv_e):
                    for X in Xs:
                        lv(X, l)
            for st in (st8, st9, st10, st11, st12, st13, st14, st15):
                for X in Xs:
                    st(X)
        P.alias_sync(list(gts[1].values()), actT[8:22])
        P.mark("L1.T%d.m1_out" % ti)
        for cg in range(4):
            slab = load_slab(w_out_o_d, cg, 8, ti == 0)
            for oc in range(2):
                ch = cg * 2 + oc
                ps = psum.get()
                for kc in range(8):
                    mm(ps.v, slab[:, kc, oc * 128:(oc + 1) * 128], actT[kc].v, kc == 0, kc == 7)
                tt("dve", xT[:, ch, :], ps.v, xT[:, ch, :], ALU.add)

    xT_view = xT_d.rearrange("(c p) t -> p c t", p=128)
    xs_view = xs_d.rearrange("(c p) t -> p c t", p=128)
    yT_view = yT_d.rearrange("(c p) t -> p c t", p=128)
    xs_tiles = [Tile("xs%d" % i, xs_view[:, :, i * TT:(i + 1) * TT]) for i in range(NT)]

    for li in range(2):
        for ti in range(NT):
            if li == 0:
                src = xT_view[:, :, ti * TT:(ti + 1) * TT]
                P.op("sp", lambda e, src=src: e.dma_start(out=xT.ap, in_=src), writes=[xT], dma=True)
            else:
                P.op("sp", lambda e, ti=ti: e.dma_start(out=xT.ap, in_=xs_tiles[ti].ap), reads=[xs_tiles[ti]],
                     writes=[xT], dma=True)
            if li == 0 and do_mix[0]:
                mixer0(ti)
            if li == 1 and do_mix[1]:
                mixer1(ti)
            ffn_ple(li, ti)
            if li == 0:
                P.op("sp", lambda e, ti=ti: e.dma_start(out=xs_tiles[ti].ap, in_=xT.ap), reads=[xT],
                     writes=[xs_tiles[ti]], dma=True)
            else:
                outs = []

                def of(kc, outs=outs):
                    t = w32.get()
                    outs.append(t)
                    return t[:, 0:TT]
                rmsnorm("final_norm", out_f32=of)
                for kc in range(8):
                    dst = yT_view[:, kc, ti * TT:(ti + 1) * TT]
                    P.op("sp", lambda e, dst=dst, t=outs[kc]: e.dma_start(out=dst, in_=t.ap[:, 0:TT]),
                         reads=[outs[kc]], dma=True, is_out=True)
    P.mark("end")
    P.finish()
    P.emit()
    nc._marks = P.marks
    es.close()
    return nc


def make_consts():
    c = np.zeros((128, NCONST, 128), np.float32)
    j = np.arange(128)
    c[:, 0, :] = np.eye(128)
    c[:, 1, :] = 1.0
    c[:, 2, :] = -(j[:, None] >= j[None, :]).astype(np.float32)
    c[:, 3, :] = (j[None, :] > j[:, None]).astype(np.float32)
    c[:, 4, :] = -1.0
    c[:, 5, :] = 0.0
    for g, w in enumerate((2, 4, 8, 16)):
        c[:, 6, g * 16:(g + 1) * 16] = 1.0 / np.minimum(np.arange(16) + 1, w)
    a_, b_ = j[:, None], j[None, :]
    c[:, C_ULE, :] = (a_ <= b_)
    c[:, C_UGT, :] = (a_ > b_)
    c[:, C_MSTRICT, :] = (a_ > b_)
    c[:, C_MINCLT, :] = (b_ >= a_)
    for l in range(7):
        bsz = 1 << l
        c[:, C_MOFF + l, :] = ((a_ // (2 * bsz) == b_ // (2 * bsz)) & ((a_ // bsz) % 2 == 1) & ((b_ // bsz) % 2 == 0))
    return c


def prep_shared(inp):
    sh = {}
    vec = np.zeros((128, NV), np.float32)

    def put(name, arr):
        o = VEC_OFF[name]
        vec[:, o:o + arr.shape[1]] = arr
    put("mix_norm_e", vec_layout(inp["mix_norm_e"][0]))
    put("pool_scale", vec_layout(inp["pool_scale"][0]))
    put("mix_norm_o", vec_layout(inp["mix_norm_o"][0]))
    for i in range(2):
        put("ffn_norm%d" % i, vec_layout(inp["ffn_norm"][i]))
        put("ple_norm%d" % i, vec_layout(inp["ple_norm"][i]))
        fc = inp["ffn_conv"][i]
        put("ffn_conv%d" % i, np.ascontiguousarray(fc.reshape(3, 44, 128).transpose(2, 1, 0).reshape(128, 132)))
        sh["w_up%d" % i] = slab_layout(inp["w_up"][i])
        sh["w_down%d" % i] = slab_layout(inp["w_down"][i])
        sh["w_pg%d" % i] = slab_layout(inp["w_ple_gate"][i])
        sh["w_ple%d" % i] = slab_layout(inp["w_ple"][i])
    put("final_norm", vec_layout(inp["final_norm"]))
    sh["w_in_e"] = slab_layout(inp["w_in_e"][0])
    sh["w_out_e"] = slab_layout(inp["w_out_e"][0])
    sh["pool_w"] = np.ascontiguousarray(inp["pool_w"][0].transpose(1, 0, 2))
    wio = inp["w_in_o"][0]
    sh["w_in_o"] = slab_layout(np.ascontiguousarray(wio[:, 0:4096]))
    sh["w_out_o"] = slab_layout(inp["w_out_o"][0])
    sh["wba"] = np.ascontiguousarray(wio[:, 4096:4112].reshape(8, 128, 16).transpose(1, 0, 2))
    sh["hv"] = np.ascontiguousarray(np.broadcast_to(
        np.concatenate([inp["dt_bias_o"][0], inp["a_log_o"][0]])[None, :], (128, 16))).astype(np.float32)
    cq = inp["conv_qkv_o"][0]
    put("conv_qkv", np.ascontiguousarray(cq.reshape(4, 24, 128).transpose(2, 1, 0).reshape(128, 96)))
    put("gdn_norm", inp["gdn_norm_o"][0].reshape(128, 1))
    sh["vecs"] = vec
    sh["consts"] = make_consts()
    return sh


def kernel(**inputs):
    inp = {k: np.asarray(v) for k, v in inputs.items()}
    x = inp["x"]
    p = inp["p"]
    B, T, _ = x.shape
    sh = prep_shared(inp)
    nc = build(T)
    in_maps = []
    for b in range(B):
        m = dict(sh)
        m["xT"] = np.ascontiguousarray(x[b].T)
        m["pT"] = np.ascontiguousarray(p[:, b].transpose(0, 2, 1))
        in_maps.append(m)
    res = run_bass_kernel_spmd(nc, in_maps, core_ids=list(range(B)))
    out = np.stack([np.ascontiguousarray(r["yT"].T) for r in res.results], axis=0)
    return out.astype(np.float32)
```

```python
import numpy as np
from contextlib import ExitStack
import concourse.bass as bass
import concourse.mybir as mybir
from concourse.bass_utils import run_bass_kernel_spmd

F32 = mybir.dt.float32
BF16 = mybir.dt.bfloat16
AF = mybir.ActivationFunctionType
ALU = mybir.AluOpType
AX = mybir.AxisListType

D = 1024
SEQ = 4096
NB = 8
FFN = 2816
PLE = 256
EPS = 1e-6
TT = 512
NDS = 8
NCONST = 19
RAW_SKIP_N = 256
C_ULE, C_UGT, C_MSTRICT, C_MINCLT, C_MOFF = 8, 9, 10, 11, 12


class Tile:
    def __init__(self, name, ap):
        self.name = name
        self.ap = ap
        self.lw = None
        self.rd = {}

    def __getitem__(self, idx):
        return V(self, self.ap[idx])

    @property
    def v(self):
        return V(self, self.ap)


class V:
    def __init__(self, tile, ap):
        self.tile = tile
        self.ap = ap

    def __getitem__(self, idx):
        return V(self.tile, self.ap[idx])


class CT:
    def __init__(self, name, ap, n):
        self.name = name
        self.ap = ap
        self.tiles = [Tile("%s_%d" % (name, c), ap[:, c, :]) for c in range(n)]

    def __getitem__(self, idx):
        p, c, f = idx
        t = self.tiles[c]
        return V(t, t.ap[p, f])

    @property
    def v(self):
        return self


class Op:
    __slots__ = ("eng", "fn", "waits", "inc", "idx", "dtok", "n")

    def __init__(self, eng, fn):
        self.eng = eng
        self.fn = fn
        self.waits = []
        self.inc = False
        self.dtok = None
        self.n = 0


ENGS = ("pe", "act", "dve", "pool", "sp")


class Prog:
    def __init__(self, nc):
        self.nc = nc
        self.ops = {e: [] for e in ENGS}
        self.ndma = {e: 0 for e in ENGS}
        self.out_toks = []
        self.marks = []

    @staticmethod
    def _tiles(lst):
        out = []
        for t in lst:
            if t is None:
                continue
            if isinstance(t, CT):
                out.extend(t.tiles)
            else:
                out.append(t.tile if isinstance(t, V) else t)
        return out

    def op(self, eng, fn, reads=(), writes=(), dma=False, is_out=False, n=0):
        R = self._tiles(reads)
        W = self._tiles(writes)
        o = Op(eng, fn)
        o.n = n
        o.idx = len(self.ops[eng])
        deps = set()
        for t in R:
            if t.lw is not None:
                if t.lw[0] == "d" or t.lw[1] != eng:
                    deps.add(t.lw)
                elif eng != "pe" and self.ops[eng][t.lw[2]].n < RAW_SKIP_N:
                    deps.add(t.lw)
            if t.name.startswith("ps"):
                for tok in t.rd.values():
                    if tok[0] == "e" and tok[1] != eng:
                        deps.add(tok)
        for t in W:
            if t.lw is not None:
                if t.lw[0] == "d" or t.lw[1] != eng or dma:
                    deps.add(t.lw)
            for tok in t.rd.values():
                if tok[0] == "d" or tok[1] != eng or dma:
                    deps.add(tok)
        if dma:
            n = self.ndma[eng]
            self.ndma[eng] = n + 1
            sem = "d_%s_%d" % (eng, n % NDS)
            if n >= NDS:
                deps.add(("d", sem, 16 * (n // NDS)))
            tok = ("d", sem, 16 * (n // NDS + 1))
            o.dtok = tok
            if is_out:
                self.out_toks.append(tok)
        else:
            tok = ("e", eng, o.idx)
        for d in deps:
            if d[0] == "e":
                self.ops[d[1]][d[2]].inc = True
            o.waits.append(d)
        for t in R:
            key = tok[1] if tok[0] == "e" else tok[1:]
            t.rd[key] = tok
        for t in W:
            t.lw = tok
            t.rd = {}
        self.ops[eng].append(o)
        return o

    def mark(self, label):
        self.marks.append((label, {e: len(self.ops[e]) for e in ENGS}))

    def finish(self):
        o = Op("sp", None)
        o.idx = len(self.ops["sp"])
        o.waits = list(self.out_toks)
        self.ops["sp"].append(o)

    def barrier(self, tiles):
        toks = []
        for e in ENGS:
            for last in reversed(self.ops[e]):
                if last.fn is not None and last.dtok is None:
                    last.inc = True
                    toks.append(("e", e, last.idx))
                    break
        for e in ENGS:
            n = self.ndma[e]
            for j in range(max(0, n - NDS), n):
                toks.append(("d", "d_%s_%d" % (e, j % NDS), 16 * (j // NDS + 1)))
        for i, t in enumerate(tiles):
            t.lw = None
            t.rd = {("bar", i): tk for i, tk in enumerate(toks)}

    def alias_sync(self, from_tiles, to_tiles):
        toks = {}
        for f in from_tiles:
            if f.lw is not None:
                toks[("lw", f.lw)] = f.lw
            for k, tk in f.rd.items():
                toks[(k, tk)] = tk
        for t in to_tiles:
            t.lw = None
            t.rd = dict(toks)

    def emit(self):
        nc = self.nc
        with ExitStack() as es:
            sems = {}
            for e in ENGS:
                sems[e] = es.enter_context(nc.semaphore("s_" + e))
                for j in range(min(NDS, self.ndma[e])):
                    k = "d_%s_%d" % (e, j)
                    sems[k] = es.enter_context(nc.semaphore(k))
            cnt = {}
            for e in ENGS:
                c = 0
                arr = []
                for o in self.ops[e]:
                    if o.inc:
                        c += 1
                    arr.append(c)
                cnt[e] = arr
            ops = self.ops

            def run(e, eng):
                seen = {}
                for o in ops[e]:
                    for d in o.waits:
                        if d[0] == "e":
                            k, val = d[1], cnt[d[1]][d[2]]
                        else:
                            k, val = d[1], d[2]
                        if seen.get(k, 0) >= val:
                            continue
                        seen[k] = val
                        eng.wait_ge(sems[k], val)
                    if o.fn is None:
                        continue
                    inst = o.fn(eng)
                    if o.dtok is not None:
                        inst.then_inc(sems[o.dtok[1]], 16)
                    elif o.inc:
                        inst.then_inc(sems[e], 1)

            block = es.enter_context(nc.Block())

            @block.tensor
            def _(eng):
                run("pe", eng)

            @block.scalar
            def _(eng):
                run("act", eng)

            @block.vector
            def _(eng):
                run("dve", eng)

            @block.gpsimd
            def _(eng):
                run("pool", eng)

            @block.sync
            def _(eng):
                run("sp", eng)


class Ctx:
    pass


DBG = 99


def pipeline(items, stages):
    n, S = len(items), len(stages)
    for i in range(n + S - 1):
        for k in range(S):
            j = i - k
            if 0 <= j < n:
                stages[k](items[j])


class Pool:
    def __init__(self, tiles):
        self.tiles = tiles
        self.i = 0

    def get(self):
        t = self.tiles[self.i % len(self.tiles)]
        self.i += 1
        return t


def slab_layout(w, width=256):
    K, N = w.shape
    return np.ascontiguousarray(w.reshape(K // 128, 128, N // width, width).transpose(2, 1, 0, 3))


def vec_layout(v):
    return np.ascontiguousarray(v.reshape(-1, 128).T)


VEC_SPECS = [
    ("mix_norm_e", 8), ("pool_scale", 4), ("mix_norm_o", 8),
    ("ffn_norm0", 8), ("ffn_norm1", 8), ("ple_norm0", 8), ("ple_norm1", 8), ("final_norm", 8),
    ("ffn_conv0", 132), ("ffn_conv1", 132), ("conv_qkv", 96), ("gdn_norm", 1),
]
VEC_OFF = {}
_o = 0
for _n, _c in VEC_SPECS:
    VEC_OFF[_n] = _o
    _o += _c
NV = _o


def build(T=SEQ, do_mix=(True, True), debug=False):
    nc = bass.Bass("TRN2", target_bir_lowering=False)
    P = Prog(nc)
    NT = T // TT
    C = Ctx()
    es = ExitStack()

    def dram_in(name, shape, dt=F32):
        return nc.dram_tensor(name, list(shape), dt, kind="ExternalInput").ap()

    xT_d = dram_in("xT", [D, T])
    pT_d = dram_in("pT", [2, PLE, T])
    vecs_d = dram_in("vecs", [128, NV])
    w_up_d = [dram_in("w_up%d" % i, [2 * FFN // 256, 128, 8, 256]) for i in range(2)]
    w_down_d = [dram_in("w_down%d" % i, [D // 256, 128, 22, 256]) for i in range(2)]
    w_pg_d = [dram_in("w_pg%d" % i, [D // 256, 128, 8, 256]) for i in range(2)]
    w_ple_d = [dram_in("w_ple%d" % i, [D // 256, 128, 2, 256]) for i in range(2)]
    consts_d = dram_in("consts", [128, NCONST, 128])
    w_in_e_d = dram_in("w_in_e", [8, 128, 8, 256])
    w_out_e_d = dram_in("w_out_e", [4, 128, 8, 256])
    pool_w_d = dram_in("pool_w", [128, 4, 128])
    w_in_o_d = dram_in("w_in_o", [16, 128, 8, 256])
    w_out_o_d = dram_in("w_out_o", [4, 128, 8, 256])
    wba_d = dram_in("wba", [128, 8, 16])
    hv_d = dram_in("hv", [128, 16])
    yT_d = nc.dram_tensor("yT", [D, T], F32, kind="ExternalOutput").ap()
    xs_d = nc.dram_tensor("xs", [D, T], F32, kind="Internal").ap()

    def sb(name, shape, dt=F32):
        return es.enter_context(nc.sbuf_tensor("sb_" + name, list(shape), dt))

    def sbt(name, shape, dt=F32):
        t = sb(name, shape, dt)
        return Tile(name, t[:])

    def dt_tile(name, ap):
        return Tile(name, ap)

    vecs = sbt("vecs", [128, NV])
    consts = sbt("consts", [128, NCONST, 128])
    cb = sbt("cb", [128, 8, 128], BF16)
    P.op("sp", lambda e: e.dma_start(out=vecs.ap, in_=vecs_d), writes=[vecs], dma=True)
    P.op("sp", lambda e: e.dma_start(out=consts.ap, in_=consts_d), writes=[consts], dma=True)
    P.op("dve", lambda e: e.tensor_copy(out=cb.ap, in_=consts.ap[:, 0:8, :]), reads=[consts], writes=[cb])
    ones_b = cb[:, 1, :]

    def vcol(name, c):
        o = VEC_OFF[name] + c
        return vecs[:, o:o + 1]

    xT = CT("xT", sb("xT", [128, 8, TT])[:], 8)
    hT = CT("hT", sb("hT", [128, 8, TT], BF16)[:], 8)
    actT_all = sb("actT", [128, 22, TT], BF16)
    actT = [Tile("actT%d" % j, actT_all[:, j, :]) for j in range(22)]
    actT2d = actT_all[:].rearrange("p c t -> p (c t)")
    actTf = actT2d.bitcast(F32)
    w32 = Pool([sbt("w32_%d" % i, [128, TT + 32]) for i in range(10)])
    bigf = sb("big", [128, 16384], F32)[:]
    big = bigf.bitcast(BF16)
    U2f = sb("U2", [128, 5120], F32)[:]
    U2b = U2f.bitcast(BF16)
    uht = [Tile("uh%d" % i, U2f[:, i * 528:(i + 1) * 528]) for i in range(4)]
    qT = Tile("qT", U2b[:, 4224:6272].rearrange("p (c t) -> p c t", c=4))
    Sacc = Pool([Tile("Sacc%d" % i, U2b[:, 6272 + i * 512:6272 + (i + 1) * 512]) for i in range(2)])
    uhalo = sbt("uhalo", [128, 4, 16])
    tmp16 = sbt("tmp16", [128, 16])
    knf = [Tile("knf%d" % h, bigf[:, h * 512:(h + 1) * 512]) for h in range(8)]
    vf = [Tile("vf%d" % h, bigf[:, 4096 + h * 512:4096 + (h + 1) * 512]) for h in range(8)]
    qnb = [Tile("qnb%d" % h, big[:, 16384 + h * 512:16384 + (h + 1) * 512]) for h in range(8)]
    knb = [Tile("knb%d" % h, big[:, 20480 + h * 512:20480 + (h + 1) * 512]) for h in range(8)]
    szT = [Tile("szT%d" % h, big[:, 24576 + h * 512:24576 + (h + 1) * 512]) for h in range(8)]
    Sst = [Tile("Sst%d" % h, bigf[:, 14336 + h * 128:14336 + (h + 1) * 128]) for h in range(8)]
    Sbb = [Tile("Sbb%d" % h, big[:, 30720 + h * 128:30720 + (h + 1) * 128]) for h in range(8)]
    sm = [Tile("sm%d" % j, bigf[:, 15872 + j * 64:15872 + (j + 1) * 64].rearrange("p (k h) -> p k h", k=8))
          for j in range(4)]
    nega = Tile("nega", bigf[:, 15872 + 256:15872 + 264])
    ssq = [Tile("ssq%d" % i, bigf[:, 15872 + 272 + i:15872 + 273 + i]) for i in range(4)]
    gt = {}
    GNAMES = ("InvA4", "InvB4", "InvTA4", "InvTB4", "qkT4", "kbd4", "kdc4", "vb4", "wT4", "vnew4")
    gts = [{}, {}]
    gts[0]["A4"] = Tile("A4_0", U2f[:, 0:512])
    gts[0]["u4"] = Tile("u4_0", U2f[:, 512:1024])
    for n_, nm in enumerate(GNAMES + ("I4", "MS4", "MI4")):
        t_ = Tile(nm + "_0", U2b[:, 2048 + n_ * 512:2048 + (n_ + 1) * 512])
        if nm in GNAMES:
            gts[0][nm] = t_
        else:
            gt[nm] = t_
    gts[1]["A4"] = Tile("A4_1", actTf[:, 8 * 256:8 * 256 + 512])
    gts[1]["u4"] = Tile("u4_1", actTf[:, 10 * 256:10 * 256 + 512])
    for n_, nm in enumerate(GNAMES):
        gts[1][nm] = Tile(nm + "_1", actT2d[:, (12 + n_) * 512:(13 + n_) * 512])
    ssq4 = [Tile("ssq4_%d" % i, bigf[:, 15872 + 280 + 4 * i:15872 + 284 + 4 * i]) for i in range(2)]
    _qh = sb("qhalo", [128, 24, 3])
    qhalo = [Tile("qhalo%d" % i, _qh[:, i, :]) for i in range(24)]
    wba = sbt("wba", [128, 8, 16], BF16)
    hv = sbt("hv", [128, 16])
    l1_tiles = knf + vf + qnb + knb + szT + Sst + Sbb + sm + [nega] + ssq4 + ssq + list(gt.values()) + list(gts[0].values())
    poolw = sbt("poolw", [128, 4, 128], BF16)
    w16 = Pool([sbt("w16_%d" % i, [128, TT], BF16) for i in range(8)])
    wsl = Pool([sbt("wsl%d" % i, [128, 8, 256], BF16) for i in range(7)])
    _ps = [Tile("ps%d" % i, es.enter_context(nc.psum_tensor("psum%d" % i, [128, 512], F32))[:])
           for i in range(8)]
    psum = Pool(_ps[0:6])
    psO = Pool(_ps[6:8])
    _fh = sb("halo", [128, 44, 2])
    halo = [Tile("halo%d" % i, _fh[:, i, :]) for i in range(44)]
    pTb = sbt("pTb", [128, 2, TT], BF16)

    dq = ["sp", "pool"]
    C.dqi = 0

    scratch = {}
    C.cast_i = 0

    def load_slab(wd, idx, nk, first, k0=0):
        key = id(wd)
        if key not in scratch:
            G, _, NK, _ = wd.shape
            sc = nc.dram_tensor("wb%d" % len(scratch), [G, 128, NK, 256], BF16, kind="Internal").ap()
            scratch[key] = (sc, {})
        sc, tiles = scratch[key]
        tk = (idx, k0)
        sl = wsl.get()
        if first:
            src = wd[idx][:, k0:k0 + nk, :]
            P.op("pool", lambda e: e.dma_start(out=sl.ap[:, 0:nk, :], in_=src), writes=[sl], dma=True)
            dst = sc[idx][:, k0:k0 + nk, :]
            tiles[tk] = Tile("wb_%d_%d_%d" % (len(scratch), idx, k0), dst)
            P.op("sp", lambda e: e.dma_start(out=dst, in_=sl.ap[:, 0:nk, :]), reads=[sl], writes=[tiles[tk]],
                 dma=True)
        else:
            t_ = tiles[tk]
            P.op("sp", lambda e: e.dma_start(out=sl.ap[:, 0:nk, :], in_=t_.ap), reads=[t_], writes=[sl], dma=True)
        return sl

    def fsz(v):
        n = 1
        for d in v.ap.shape[1:]:
            n *= int(d)
        return n

    def mm(out, lhsT, rhs, start, stop, skip=False):
        P.op("pe", lambda e: e.matmul(out.ap, lhsT=lhsT.ap, rhs=rhs.ap, start=start, stop=stop,
                                      skip_group_check=skip),
             reads=[lhsT, rhs], writes=[out])

    def act(out, in_, func, reads=(), **kw):
        kw2 = {k: (v.ap if isinstance(v, V) else v) for k, v in kw.items()}
        extra = [v for v in kw.values() if isinstance(v, V)]
        P.op("act", lambda e: e.activation(out=out.ap, in_=in_.ap, func=func, **kw2),
             reads=[in_] + extra + list(reads), writes=[out], n=fsz(out))

    def tt(eng, out, in0, in1, op):
        P.op(eng, lambda e: e.tensor_tensor(out=out.ap, in0=in0.ap, in1=in1.ap, op=op),
             reads=[in0, in1], writes=[out], n=fsz(out))

    def ts(eng, out, in0, s1, s2, op0, op1=None):
        a1 = s1.ap if isinstance(s1, V) else s1
        a2 = s2.ap if isinstance(s2, V) else s2
        rd = [in0] + [s for s in (s1, s2) if isinstance(s, V)]
        if op1 is None:
            P.op(eng, lambda e: e.tensor_scalar(out=out.ap, in0=in0.ap, scalar1=a1, scalar2=None, op0=op0),
                 reads=rd, writes=[out], n=fsz(out))
        else:
            P.op(eng, lambda e: e.tensor_scalar(out=out.ap, in0=in0.ap, scalar1=a1, scalar2=a2, op0=op0, op1=op1),
                 reads=rd, writes=[out], n=fsz(out))

    def stt(out, in0, scalar, in1, op0, op1):
        a = scalar.ap if isinstance(scalar, V) else scalar
        rd = [in0, in1] + ([scalar] if isinstance(scalar, V) else [])
        P.op("dve", lambda e: e.scalar_tensor_tensor(out=out.ap, in0=in0.ap, scalar=a, in1=in1.ap, op0=op0, op1=op1),
             reads=rd, writes=[out], n=fsz(out))

    def copy(eng, out, in_):
        P.op(eng, lambda e: e.tensor_copy(out=out.ap, in_=in_.ap), reads=[in_], writes=[out], n=fsz(out))

    def rmsnorm(gname, out_tile=None, out_f32=None):
        ps = psum.get()
        for kc in range(8):
            act(hT[:, kc, :], xT[:, kc, :], AF.Square)
            mm(ps.v, ones_b, hT[:, kc, :], kc == 0, kc == 7)
        rs = w32.get()
        act(rs[:, 0:TT], ps.v, AF.Ln, scale=1.0 / D, bias=EPS)
        act(rs[:, 0:TT], rs[:, 0:TT], AF.Exp, scale=-0.5)
        for kc in range(8):
            if out_f32 is not None:
                stt(out_f32(kc), xT[:, kc, :], vcol(gname, kc), rs[:, 0:TT], ALU.mult, ALU.mult)
            else:
                stt(hT[:, kc, :], xT[:, kc, :], vcol(gname, kc), rs[:, 0:TT], ALU.mult, ALU.mult)

    def ffn_ple(li, ti):
        P.mark("L%d.T%d.ffn_up" % (li, ti))
        for kc in range(2):
            srcp = pT_d[li][kc * 128:(kc + 1) * 128, ti * TT:(ti + 1) * TT]
            P.op("pool", lambda e, kc=kc, srcp=srcp: e.dma_start(out=pTb.ap[:, kc, :], in_=srcp), writes=[pTb],
                 dma=True)
        rmsnorm("ffn_norm%d" % li)
        cw = "ffn_conv%d" % li

        def conv_chunk(f, ps):
            raw = w32.get()
            cv = w32.get()
            if ti == 0:
                P.op("pool", lambda e: e.memset(raw.ap[:, 0:2], 0.0), writes=[raw])
            else:
                copy("pool", raw[:, 0:2], halo[f].v)
            act(raw[:, 2:2 + TT], ps.v, AF.Copy)
            act(cv[:, 0:TT], ps.v, AF.Copy, scale=vcol(cw, f * 3 + 2))
            stt(cv[:, 0:TT], raw[:, 1:1 + TT], vcol(cw, f * 3 + 1), cv[:, 0:TT], ALU.mult, ALU.add)
            stt(cv[:, 0:TT], raw[:, 0:TT], vcol(cw, f * 3 + 0), cv[:, 0:TT], ALU.mult, ALU.add)
            copy("pool", halo[f].v, raw[:, TT:TT + 2])
            return cv

        fslabs = {}

        def fS0(it):
            j = it["j"]
            sg, oc = j // 2, j % 2
            if oc == 0:
                fslabs[sg] = [load_slab(w_up_d[li], sg, 8, ti == 0), load_slab(w_up_d[li], FFN // 256 + sg, 8, ti == 0)]
            it["pss"] = []
            for s_i in range(2):
                ps = psum.get()
                for kc in range(8):
                    mm(ps.v, fslabs[sg][s_i][:, kc, oc * 128:(oc + 1) * 128], hT[:, kc, :], kc == 0, kc == 7)
                it["pss"].append(ps)

        def fS1(it):
            j = it["j"]
            it["cg"] = conv_chunk(j, it["pss"][0])
            it["cvv"] = conv_chunk(22 + j, it["pss"][1])

        def fS2(it):
            j, cg, cvv = it["j"], it["cg"], it["cvv"]
            act(cg[:, 0:TT], cg[:, 0:TT], AF.Silu)
            tt("dve", actT[j].v, cg[:, 0:TT], cvv[:, 0:TT], ALU.mult)

        pipeline([{"j": j} for j in range(22)], [fS0, fS1, fS2])
        P.mark("L%d.T%d.ffn_down" % (li, ti))
        for cg in range(D // 256):
            pss = [psum.get(), psum.get()]
            for kg in range(0, 22, 8):
                nk = min(8, 22 - kg)
                slab = load_slab(w_down_d[li], cg, nk, ti == 0, k0=kg)
                for oc in range(2):
                    for k in range(nk):
                        mm(pss[oc].v, slab[:, k, oc * 128:(oc + 1) * 128], actT[kg + k].v,
                           kg + k == 0, kg + k == 21)
            for oc in range(2):
                c = cg * 2 + oc
                tt("dve", xT[:, c, :], pss[oc].v, xT[:, c, :], ALU.add)
        P.mark("L%d.T%d.ple" % (li, ti))
        rmsnorm("ple_norm%d" % li)
        pslabs = {}

        def qS0(it):
            c = it["c"]
            cg, oc = c // 2, c % 2
            if oc == 0:
                pslabs[cg] = (load_slab(w_pg_d[li], cg, 8, ti == 0), load_slab(w_ple_d[li], cg, 2, ti == 0))
            sg_, sp_ = pslabs[cg]
            pg = psum.get()
            pp = psum.get()
            it["pg"], it["pp"] = pg, pp
            for kc in range(8):
                mm(pg.v, sg_[:, kc, oc * 128:(oc + 1) * 128], hT[:, kc, :], kc == 0, kc == 7)
            for kc in range(2):
                mm(pp.v, sp_[:, kc, oc * 128:(oc + 1) * 128], pTb[:, kc, :], kc == 0, kc == 1)

        def qS1(it):
            g = w32.get()
            it["g"] = g
            act(g[:, 0:TT], it["pg"].v, AF.Sigmoid)

        def qS2(it):
            c, g = it["c"], it["g"]
            tt("dve", g[:, 0:TT], it["pp"].v, g[:, 0:TT], ALU.mult)
            tt("dve", xT[:, c, :], g[:, 0:TT], xT[:, c, :], ALU.add)

        pipeline([{"c": c} for c in range(8)], [qS0, qS1, qS2])

    kT = Tile("kT", big[:, 0:4 * T].rearrange("p (c t) -> p c t", c=4))
    Vc = Tile("Vc", big[:, 4 * T:4 * T + (T // 128) * 512].rearrange("p (b f) -> p b f", f=512))
    C.mix0_init = False

    def mixer0(ti):
        t0 = ti * TT
        if not C.mix0_init:
            C.mix0_init = True
            pwf = w32.get()
            P.op("sp", lambda e: e.dma_start(out=pwf.ap[:, 0:512].rearrange("p (g d) -> p g d", g=4), in_=pool_w_d),
                 writes=[pwf], dma=True)
            copy("dve", poolw.v, V(pwf, pwf.ap[:, 0:512].rearrange("p (g d) -> p g d", g=4)))
        P.mark("L0.T%d.m0_proj" % ti)
        rmsnorm("mix_norm_e")
        uh = []
        for sgi in range(8):
            slab = load_slab(w_in_e_d, sgi, 8, ti == 0)
            if sgi < 6:
                for oc in range(2):
                    ch = sgi * 2 + oc
                    ps = psum.get()
                    for kc in range(8):
                        mm(ps.v, slab[:, kc, oc * 128:(oc + 1) * 128], hT[:, kc, :], kc == 0, kc == 7)
                    if ch < 4:
                        u = uht[ch]
                        uh.append(u)
                        if ti == 0:
                            P.op("pool", lambda e, u=u: e.memset(u.ap[:, 0:16], 0.0), writes=[u])
                        else:
                            copy("pool", u[:, 0:16], uhalo[:, ch, :])
                        act(u[:, 16:16 + TT], ps.v, AF.Copy)
                        copy("pool", uhalo[:, ch, :], u[:, TT:TT + 16])
                    elif ch < 8:
                        act(qT[:, ch - 4, :], ps.v, AF.Copy, scale=0.125)
                    else:
                        copy("dve", kT[:, ch - 8, t0:t0 + TT], ps.v)
            else:
                half = sgi - 6
                for j in range(4):
                    ps = psum.get()
                    for kc in range(8):
                        mm(ps[:, 0:256], hT[:, kc, j * 128:(j + 1) * 128], slab[:, kc, :], kc == 0, kc == 7)
                    copy("dve", Vc[:, ti * 4 + j, half * 256:(half + 1) * 256], ps[:, 0:256])
        P.mark("L0.T%d.m0_pool" % ti)
        for g in range(4):
            w = 2 << g
            u = uh[g]
            s_prev = u
            lo = 0
            for l in range(g + 1):
                sh = 1 << l
                lo += sh
                s_new = w32.get()
                tt("dve", s_new[:, lo:16 + TT], s_prev[:, lo:16 + TT], s_prev[:, lo - sh:16 + TT - sh], ALU.add)
                s_prev = s_new
            y = w16.get()
            stt(y.v, s_prev[:, 16:16 + TT], 1.0 / w, u[:, 16:16 + TT], ALU.mult, ALU.subtract)
            if ti == 0:
                tmp = tmp16
                tt("dve", tmp[:, 0:16], s_prev[:, 16:32], consts[:, 6, g * 16:(g + 1) * 16], ALU.mult)
                tt("dve", y[:, 0:16], tmp[:, 0:16], u[:, 16:32], ALU.subtract)
            ps = psum.get()
            mm(ps.v, poolw[:, g, :], y.v, True, True)
            act(actT[g].v, ps.v, AF.Copy, scale=vcol("pool_scale", g))
        P.mark("L0.T%d.m0_attn" % ti)
        nkb = (t0 + TT) // 128
        steps = []
        for c in range(4):
            for kb in range(nkb - 1, -1, -1):
                steps.append({"c": c, "kb": kb, "first": kb == nkb - 1})
        cur = {}

        def stA(st):
            c, kb = st["c"], st["kb"]
            if st["first"]:
                cur["ops"] = psO.get()
                mm(cur["ops"].v, cb[:, 5, :], qT[:, c, :], True, True)
                cur["S"] = [Sacc.get(), Sacc.get()]
            st["ops"] = cur["ops"]
            st["S"] = cur["S"]
            r = kb - t0 // 128
            q_lo = max(r, 0) * 128
            st["r"], st["q_lo"] = r, q_lo
            st["zp"] = [psum.get(), psum.get()]
            for hh in range(2):
                r0 = hh * 64
                mm(st["zp"][hh][:, q_lo:TT], kT[r0:r0 + 64, c, kb * 128:(kb + 1) * 128], qT[r0:r0 + 64, c, q_lo:TT],
                   True, True)
            st["Lp"] = []
            for hh in range(2):
                E = w32.get()
                act(E[:, q_lo:TT], st["zp"][hh][:, q_lo:TT], AF.Exp)
                Lp = w16.get()
                st["Lp"].append(Lp)
                act(Lp[:, q_lo:TT], E[:, q_lo:TT], AF.Ln, bias=1.0)
                if r >= 0:
                    tt("pool", Lp[:, q_lo:q_lo + 128], Lp[:, q_lo:q_lo + 128], cb[:, 3, :], ALU.mult)

        def stB(st):
            q_lo, r, kb, first = st["q_lo"], st["r"], st["kb"], st["first"]
            for hh in range(2):
                zp, Lp, S = st["zp"][hh], st["Lp"][hh], st["S"][hh]
                mm(zp[:, q_lo:TT], cb[:, 2, :], Lp[:, q_lo:TT], False, first, skip=True)
                if not first:
                    mm(zp[:, q_lo:TT], cb[:, 4, :], S[:, q_lo:TT], False, True, skip=True)
            st["A"] = []
            for hh in range(2):
                zp, Lp, S = st["zp"][hh], st["Lp"][hh], st["S"][hh]
                A = w16.get()
                st["A"].append(A)
                act(A[:, q_lo:TT], zp[:, q_lo:TT], AF.Exp)
                if r >= 0:
                    tt("pool", A[:, q_lo:q_lo + 128], A[:, q_lo:q_lo + 128], cb[:, 3, :], ALU.mult)
                if kb > 0:
                    if first:
                        if q_lo > 0:
                            P.op("pool", lambda e, S=S, q_lo=q_lo: e.memset(S.ap[:, 0:q_lo], 0.0), writes=[S])
                        copy("pool", S[:, q_lo:TT], Lp[:, q_lo:TT])
                    else:
                        tt("pool", S[:, q_lo:TT], S[:, q_lo:TT], Lp[:, q_lo:TT], ALU.add)

        def stC(st):
            c, kb, q_lo, ops_ = st["c"], st["kb"], st["q_lo"], st["ops"]
            for hh in range(2):
                r0 = hh * 64
                A = st["A"][hh]
                last = True
                P.op("pe", lambda e, A=A, r0=r0, last=last: e.matmul(
                    ops_.ap[r0:r0 + 64, q_lo:TT], lhsT=Vc.ap[:, kb, c * 128 + r0:c * 128 + r0 + 64],
                    rhs=A.ap[:, q_lo:TT], start=False, stop=last, tile_position=(0, r0), skip_group_check=True),
                    reads=[Vc, A], writes=[ops_])
            if kb == 0:
                copy("dve", actT[4 + c].v, ops_.v)

        pipeline(steps, [stA, stB, stC])
        P.mark("L0.T%d.m0_out" % ti)
        for cg in range(4):
            slab = load_slab(w_out_e_d, cg, 8, ti == 0)
            for oc in range(2):
                ch = cg * 2 + oc
                ps = psum.get()
                for kc in range(8):
                    mm(ps.v, slab[:, kc, oc * 128:(oc + 1) * 128], actT[kc].v, kc == 0, kc == 7)
                tt("dve", xT[:, ch, :], ps.v, xT[:, ch, :], ALU.add)

    C.mix1_init = False
    I_f = consts[:, 0, :]
    ones_f = consts[:, 1, :]

    def tr(out, in_):
        P.op("pe", lambda e: e.transpose(out=out.ap, in_=in_.ap, identity=I_f.ap), reads=[in_, I_f], writes=[out])

    def mixer1(ti):
        if not C.mix1_init:
            C.mix1_init = True
            P.barrier(l1_tiles)
            wf = w32.get()
            P.op("sp", lambda e: e.dma_start(out=wf.ap[:, 0:128].rearrange("p (k c) -> p k c", k=8), in_=wba_d),
                 writes=[wf], dma=True)
            copy("dve", wba.v, V(wf, wf.ap[:, 0:128].rearrange("p (k c) -> p k c", k=8)))
            P.op("sp", lambda e: e.dma_start(out=hv.ap, in_=hv_d), writes=[hv], dma=True)
            act(nega.v, hv[:, 8:16], AF.Exp)
            ts("dve", nega.v, nega.v, -1.0, None, ALU.mult)
            for hi in range(4):
                copy("pool", V(gt["I4"], gt["I4"].ap[:, hi * 128:(hi + 1) * 128]), cb[:, 0, :])
                copy("pool", V(gt["MS4"], gt["MS4"].ap[:, hi * 128:(hi + 1) * 128]), consts[:, C_MSTRICT, :])
                copy("pool", V(gt["MI4"], gt["MI4"].ap[:, hi * 128:(hi + 1) * 128]), consts[:, C_MINCLT, :])
            for h in range(8):
                P.op("pool", lambda e, h=h: e.memset(Sst[h].ap, 0.0), writes=[Sst[h]])
                P.op("pool", lambda e, h=h: e.memset(Sbb[h].ap, 0.0), writes=[Sbb[h]])
        P.mark("L1.T%d.m1_proj" % ti)
        rmsnorm("mix_norm_o")
        items = [{"ch": ch} for ch in range(32)]
        slabs = {}

        def pS0(it):
            ch = it["ch"]
            sgi, oc = ch // 2, ch % 2
            if oc == 0:
                slabs[sgi] = load_slab(w_in_o_d, sgi, 8, ti == 0)
            slab = slabs[sgi]
            ps = psum.get()
            it["ps"] = ps
            for kc in range(8):
                mm(ps.v, slab[:, kc, oc * 128:(oc + 1) * 128], hT[:, kc, :], kc == 0, kc == 7)

        def pS1(it):
            ch, ps = it["ch"], it["ps"]
            if ch >= 24:
                act(szT[ch - 24].v, ps.v, AF.Silu)
                return
            raw = w32.get()
            cv = w32.get()
            it["cv"] = cv
            if ti == 0:
                P.op("pool", lambda e, raw=raw: e.memset(raw.ap[:, 0:3], 0.0), writes=[raw])
            else:
                copy("pool", raw[:, 0:3], qhalo[ch].v)
            act(cv[:, 0:TT], ps.v, AF.Copy, scale=vcol("conv_qkv", ch * 4 + 3))
            copy("dve", raw[:, 3:3 + TT], ps.v)
            for i in (2, 1, 0):
                stt(cv[:, 0:TT], raw[:, i:i + TT], vcol("conv_qkv", ch * 4 + i), cv[:, 0:TT], ALU.mult, ALU.add)
            copy("pool", qhalo[ch].v, raw[:, TT:TT + 3])

        def pS2(it):
            ch = it["ch"]
            if ch >= 24:
                return
            cv = it["cv"]
            if ch >= 16:
                act(vf[ch - 16].v, cv[:, 0:TT], AF.Silu)
                return
            act(cv[:, 0:TT], cv[:, 0:TT], AF.Silu)
            sqb = w16.get()
            tt("pool", sqb.v, cv[:, 0:TT], cv[:, 0:TT], ALU.mult)
            ps2 = psum.get()
            it["ps2"] = ps2
            mm(ps2.v, ones_b, sqb.v, True, True)

        def pS3(it):
            ch = it["ch"]
            if ch >= 16:
                return
            cv, ps2 = it["cv"], it["ps2"]
            rn = w32.get()
            act(rn[:, 0:TT], ps2.v, AF.Ln, bias=EPS)
            act(rn[:, 0:TT], rn[:, 0:TT], AF.Exp, scale=-0.5)
            if ch < 8:
                stt(qnb[ch].v, cv[:, 0:TT], float(128 ** -0.5), rn[:, 0:TT], ALU.mult, ALU.mult)
            else:
                tt("dve", knf[ch - 8].v, cv[:, 0:TT], rn[:, 0:TT], ALU.mult)
                copy("pool", knb[ch - 8].v, knf[ch - 8].v)

        pipeline(items, [pS0, pS1, pS2, pS3])
        P.mark("L1.T%d.m1_gates" % ti)
        if DBG < 2:
            return
        for j in range(4):
            s_ = sm[j]
            ps = psum.get()
            for kc in range(8):
                mm(ps[:, 0:16], hT[:, kc, j * 128:(j + 1) * 128], wba[:, kc, :], kc == 0, kc == 7)
            act(s_[:, 0, :], ps[:, 0:8], AF.Sigmoid)
            tt("dve", s_[:, 7, :], ps[:, 8:16], hv[:, 0:8], ALU.add)
            act(s_[:, 7, :], s_[:, 7, :], AF.Exp)
            act(s_[:, 7, :], s_[:, 7, :], AF.Ln, bias=1.0)
            tt("dve", s_[:, 1, :], s_[:, 7, :], nega.v, ALU.mult)
            ps2 = psum.get()
            mm(ps2[:, 0:8], consts[:, C_ULE, :], s_[:, 1, :], True, True)
            mm(ps2[:, 8:16], ones_f, s_[:, 1, :], True, True)
            act(s_[:, 2, :], ps2[:, 0:8], AF.Copy)
            act(s_[:, 3, :], ps2[:, 0:8], AF.Exp)
            act(s_[:, 4, :], ps2[:, 8:16], AF.Exp)
            tt("dve", s_[:, 7, :], ps2[:, 8:16], s_[:, 2, :], ALU.subtract)
            act(s_[:, 5, :], s_[:, 7, :], AF.Exp)
            tt("dve", s_[:, 6, :], s_[:, 0, :], s_[:, 3, :], ALU.mult)
        P.mark("L1.T%d.m1_chunks" % ti)
        if DBG < 3:
            return
        def hs_(x, hi):
            return x[:, hi * 128:(hi + 1) * 128]

        P.alias_sync(actT[8:22], list(gts[1].values()))
        for j in range(4):
            s_ = sm[j]
            cs = slice(j * 128, (j + 1) * 128)
            Xs = []
            for hg in range(2):
                Xs.append({"hg": hg, "g": gts[hg], "hs": [(hi, hg * 4 + hi) for hi in range(4)]})

            def gv(X, name, hi):
                t_ = X["g"][name]
                return V(t_, t_.ap[:, hi * 128:(hi + 1) * 128])

            def st1(X):
                X["gU4"], X["gL4"] = w32.get(), w32.get()
                for hi, h in X["hs"]:
                    ts("dve", hs_(X["gU4"], hi), consts[:, C_UGT, :], s_[:, 1, h:h + 1], None, ALU.mult)
                    ts("dve", hs_(X["gL4"], hi), consts[:, C_ULE, :], s_[:, 1, h:h + 1], None, ALU.mult)

            def st2(X):
                X["psD1"], X["psD2"] = psum.get(), psum.get()
                for hi, h in X["hs"]:
                    mm(hs_(X["psD1"], hi), consts[:, C_ULE, :], hs_(X["gU4"], hi), True, True)
                for hi, h in X["hs"]:
                    mm(hs_(X["psD2"], hi), consts[:, C_UGT, :], hs_(X["gL4"], hi), True, True)

            def st3(X):
                X["decS4"], X["decTI4"] = w32.get(), w32.get()
                act(X["decS4"][:, 0:512], X["psD1"].v, AF.Exp)
                tt("pool", X["decS4"][:, 0:512], X["decS4"][:, 0:512], gt["MS4"].v, ALU.mult)
                act(X["decTI4"][:, 0:512], X["psD2"].v, AF.Exp)
                tt("pool", X["decTI4"][:, 0:512], X["decTI4"][:, 0:512], gt["MI4"].v, ALU.mult)

            def st4(X):
                X["psK1"], X["psK2"] = psum.get(), psum.get()
                for hi, h in X["hs"]:
                    mm(hs_(X["psK1"], hi), knb[h][:, cs], knb[h][:, cs], True, True)
                for hi, h in X["hs"]:
                    mm(hs_(X["psK2"], hi), knb[h][:, cs], qnb[h][:, cs], True, True)

            def st5(X):
                for hi, h in X["hs"]:
                    stt(gv(X, "A4", hi), hs_(X["psK1"], hi), s_[:, 0, h:h + 1], hs_(X["decS4"], hi), ALU.mult, ALU.mult)
                tt("dve", X["g"]["qkT4"].v, X["psK2"].v, X["decTI4"][:, 0:512], ALU.mult)

            def st6(X):
                X["psT1"], X["psT2"] = psum.get(), psum.get()
                for hi, h in X["hs"]:
                    tr(hs_(X["psT1"], hi), knf[h][:, cs])
                for hi, h in X["hs"]:
                    tr(hs_(X["psT2"], hi), vf[h][:, cs])

            def st7(X):
                for hi, h in X["hs"]:
                    ts("dve", gv(X, "kbd4", hi), hs_(X["psT1"], hi), s_[:, 6, h:h + 1], None, ALU.mult)
                    ts("dve", gv(X, "kdc4", hi), hs_(X["psT1"], hi), s_[:, 5, h:h + 1], None, ALU.mult)
                    act(gv(X, "vb4", hi), hs_(X["psT2"], hi), AF.Copy, scale=s_[:, 0, h:h + 1])
                X["Inv"], X["InvT"] = gt["I4"], gt["I4"]

            def lv_a(X, l):
                X["Aoff4"] = w16.get()
                for hi, h in X["hs"]:
                    tt("pool", hs_(X["Aoff4"], hi), gv(X, "A4", hi), consts[:, C_MOFF + l, :], ALU.mult)

            def lv_b(X, l):
                X["psY"] = psum.get()
                for hi, h in X["hs"]:
                    mm(hs_(X["psY"], hi), hs_(X["Aoff4"], hi), hs_(X["InvT"], hi), True, True)

            def lv_c(X, l):
                X["ImY4"] = w16.get()
                tt("dve", X["ImY4"].v, gt["I4"].v, X["psY"].v, ALU.subtract)

            def lv_d(X, l):
                if l < 6:
                    X["psI1"] = psum.get()
                    for hi, h in X["hs"]:
                        mm(hs_(X["psI1"], hi), hs_(X["ImY4"], hi), hs_(X["Inv"], hi), True, True)
                X["psI2"] = psum.get()
                for hi, h in X["hs"]:
                    mm(hs_(X["psI2"], hi), hs_(X["Inv"], hi), hs_(X["ImY4"], hi), True, True)

            def lv_e(X, l):
                nT = X["g"]["InvTA4" if l % 2 == 0 else "InvTB4"]
                if l < 6:
                    nI = X["g"]["InvA4" if l % 2 == 0 else "InvB4"]
                    act(nI.v, X["psI1"].v, AF.Copy)
                else:
                    nI = None
                copy("dve", nT.v, X["psI2"].v)
                X["Inv"], X["InvT"] = nI, nT

            def st8(X):
                X["psU1"], X["psU2"] = psum.get(), psum.get()
                IT = X["InvT"]
                for hi, h in X["hs"]:
                    mm(hs_(X["psU1"], hi), hs_(IT, hi), gv(X, "vb4", hi), True, True)
                for hi, h in X["hs"]:
                    mm(hs_(X["psU2"], hi), gv(X, "kbd4", hi), hs_(IT, hi), True, True)

            def st9(X):
                act(X["g"]["u4"].v, X["psU1"].v, AF.Copy)
                copy("dve", X["g"]["wT4"].v, X["psU2"].v)

            def st10(X):
                X["psP1"], X["psP2"] = psum.get(), psum.get()
                for hi, h in X["hs"]:
                    mm(hs_(X["psP1"], hi), gv(X, "wT4", hi), Sbb[h].v, True, True)
                for hi, h in X["hs"]:
                    mm(hs_(X["psP2"], hi), qnb[h][:, cs], Sbb[h].v, True, True)

            def st11(X):
                tt("dve", X["g"]["vnew4"].v, X["g"]["u4"].v, X["psP1"].v, ALU.subtract)
                X["o1"], X["o"] = w32.get(), w32.get()
                for hi, h in X["hs"]:
                    act(hs_(X["o1"], hi), hs_(X["psP2"], hi), AF.Copy, scale=s_[:, 3, h:h + 1])

            def st12(X):
                X["psQ1"], X["psQ2"] = psum.get(), psum.get()
                for hi, h in X["hs"]:
                    mm(hs_(X["psQ1"], hi), gv(X, "qkT4", hi), gv(X, "vnew4", hi), True, True)
                for hi, h in X["hs"]:
                    mm(hs_(X["psQ2"], hi), gv(X, "kdc4", hi), gv(X, "vnew4", hi), True, True)

            def st13(X):
                o1, o = X["o1"], X["o"]
                tt("dve", o[:, 0:512], o1[:, 0:512], X["psQ1"].v, ALU.add)
                for hi, h in X["hs"]:
                    stt(Sst[h].v, Sst[h].v, s_[:, 4, h:h + 1], hs_(X["psQ2"], hi), ALU.mult, ALU.add)
                    copy("pool", Sbb[h].v, Sst[h].v)
                tt("dve", o1[:, 0:512], o[:, 0:512], o[:, 0:512], ALU.mult)
                sq_ = ssq4[X["hg"]]
                P.op("dve", lambda e, o1=o1, sq_=sq_: e.tensor_reduce(
                    out=sq_.ap, in_=o1.ap[:, 0:512].rearrange("p (h e) -> p h e", h=4), axis=AX.X, op=ALU.add),
                    reads=[o1], writes=[sq_])
                act(sq_.v, sq_.v, AF.Ln, scale=1.0 / 128, bias=EPS)
                act(sq_.v, sq_.v, AF.Exp, scale=-0.5)

            def st14(X):
                o = X["o"]
                sq_ = ssq4[X["hg"]]
                for hi, h in X["hs"]:
                    ts("dve", hs_(o, hi), hs_(o, hi), sq_[:, hi:hi + 1], None, ALU.mult)
                X["psT3"] = psum.get()
                for hi, h in X["hs"]:
                    tr(hs_(X["psT3"], hi), hs_(o, hi))

            def st15(X):
                for hi, h in X["hs"]:
                    stt(actT[h][:, cs], hs_(X["psT3"], hi), vcol("gdn_norm", 0), szT[h][:, cs], ALU.mult, ALU.mult)

            for st in (st1, st2, st3, st4, st5, st6, st7):
                for X in Xs:
                    st(X)
            for l in range(7):
                for lv in (lv_a, lv_b, lv_c, lv_d, lv_e):
                    for X in Xs:
                        lv(X, l)
            for st in (st8, st9, st10, st11, st12, st13, st14, st15):
                for X in Xs:
                    st(X)
        P.alias_sync(list(gts[1].values()), actT[8:22])
        P.mark("L1.T%d.m1_out" % ti)
        for cg in range(4):
            slab = load_slab(w_out_o_d, cg, 8, ti == 0)
            for oc in range(2):
                ch = cg * 2 + oc
                ps = psum.get()
                for kc in range(8):
                    mm(ps.v, slab[:, kc, oc * 128:(oc + 1) * 128], actT[kc].v, kc == 0, kc == 7)
                tt("dve", xT[:, ch, :], ps.v, xT[:, ch, :], ALU.add)

    xT_view = xT_d.rearrange("(c p) t -> p c t", p=128)
    xs_view = xs_d.rearrange("(c p) t -> p c t", p=128)
    yT_view = yT_d.rearrange("(c p) t -> p c t", p=128)
    xs_tiles = [Tile("xs%d" % i, xs_view[:, :, i * TT:(i + 1) * TT]) for i in range(NT)]

    for li in range(2):
        for ti in range(NT):
            if li == 0:
                src = xT_view[:, :, ti * TT:(ti + 1) * TT]
                P.op("sp", lambda e, src=src: e.dma_start(out=xT.ap, in_=src), writes=[xT], dma=True)
            else:
                P.op("sp", lambda e, ti=ti: e.dma_start(out=xT.ap, in_=xs_tiles[ti].ap), reads=[xs_tiles[ti]],
                     writes=[xT], dma=True)
            if li == 0 and do_mix[0]:
                mixer0(ti)
            if li == 1 and do_mix[1]:
                mixer1(ti)
            ffn_ple(li, ti)
            if li == 0:
                P.op("sp", lambda e, ti=ti: e.dma_start(out=xs_tiles[ti].ap, in_=xT.ap), reads=[xT],
                     writes=[xs_tiles[ti]], dma=True)
            else:
                outs = []

                def of(kc, outs=outs):
                    t = w32.get()
                    outs.append(t)
                    return t[:, 0:TT]
                rmsnorm("final_norm", out_f32=of)
                for kc in range(8):
                    dst = yT_view[:, kc, ti * TT:(ti + 1) * TT]
                    P.op("sp", lambda e, dst=dst, t=outs[kc]: e.dma_start(out=dst, in_=t.ap[:, 0:TT]),
                         reads=[outs[kc]], dma=True, is_out=True)
    P.mark("end")
    P.finish()
    P.emit()
    nc._marks = P.marks
    es.close()
    return nc


def make_consts():
    c = np.zeros((128, NCONST, 128), np.float32)
    j = np.arange(128)
    c[:, 0, :] = np.eye(128)
    c[:, 1, :] = 1.0
    c[:, 2, :] = -(j[:, None] >= j[None, :]).astype(np.float32)
    c[:, 3, :] = (j[None, :] > j[:, None]).astype(np.float32)
    c[:, 4, :] = -1.0
    c[:, 5, :] = 0.0
    for g, w in enumerate((2, 4, 8, 16)):
        c[:, 6, g * 16:(g + 1) * 16] = 1.0 / np.minimum(np.arange(16) + 1, w)
    a_, b_ = j[:, None], j[None, :]
    c[:, C_ULE, :] = (a_ <= b_)
    c[:, C_UGT, :] = (a_ > b_)
    c[:, C_MSTRICT, :] = (a_ > b_)
    c[:, C_MINCLT, :] = (b_ >= a_)
    for l in range(7):
        bsz = 1 << l
        c[:, C_MOFF + l, :] = ((a_ // (2 * bsz) == b_ // (2 * bsz)) & ((a_ // bsz) % 2 == 1) & ((b_ // bsz) % 2 == 0))
    return c


def prep_shared(inp):
    sh = {}
    vec = np.zeros((128, NV), np.float32)

    def put(name, arr):
        o = VEC_OFF[name]
        vec[:, o:o + arr.shape[1]] = arr
    put("mix_norm_e", vec_layout(inp["mix_norm_e"][0]))
    put("pool_scale", vec_layout(inp["pool_scale"][0]))
    put("mix_norm_o", vec_layout(inp["mix_norm_o"][0]))
    for i in range(2):
        put("ffn_norm%d" % i, vec_layout(inp["ffn_norm"][i]))
        put("ple_norm%d" % i, vec_layout(inp["ple_norm"][i]))
        fc = inp["ffn_conv"][i]
        put("ffn_conv%d" % i, np.ascontiguousarray(fc.reshape(3, 44, 128).transpose(2, 1, 0).reshape(128, 132)))
        sh["w_up%d" % i] = slab_layout(inp["w_up"][i])
        sh["w_down%d" % i] = slab_layout(inp["w_down"][i])
        sh["w_pg%d" % i] = slab_layout(inp["w_ple_gate"][i])
        sh["w_ple%d" % i] = slab_layout(inp["w_ple"][i])
    put("final_norm", vec_layout(inp["final_norm"]))
    sh["w_in_e"] = slab_layout(inp["w_in_e"][0])
    sh["w_out_e"] = slab_layout(inp["w_out_e"][0])
    sh["pool_w"] = np.ascontiguousarray(inp["pool_w"][0].transpose(1, 0, 2))
    wio = inp["w_in_o"][0]
    sh["w_in_o"] = slab_layout(np.ascontiguousarray(wio[:, 0:4096]))
    sh["w_out_o"] = slab_layout(inp["w_out_o"][0])
    sh["wba"] = np.ascontiguousarray(wio[:, 4096:4112].reshape(8, 128, 16).transpose(1, 0, 2))
    sh["hv"] = np.ascontiguousarray(np.broadcast_to(
        np.concatenate([inp["dt_bias_o"][0], inp["a_log_o"][0]])[None, :], (128, 16))).astype(np.float32)
    cq = inp["conv_qkv_o"][0]
    put("conv_qkv", np.ascontiguousarray(cq.reshape(4, 24, 128).transpose(2, 1, 0).reshape(128, 96)))
    put("gdn_norm", inp["gdn_norm_o"][0].reshape(128, 1))
    sh["vecs"] = vec
    sh["consts"] = make_consts()
    return sh


def kernel(**inputs):
    inp = {k: np.asarray(v) for k, v in inputs.items()}
    x = inp["x"]
    p = inp["p"]
    B, T, _ = x.shape
    sh = prep_shared(inp)
    nc = build(T)
    in_maps = []
    for b in range(B):
        m = dict(sh)
        m["xT"] = np.ascontiguousarray(x[b].T)
        m["pT"] = np.ascontiguousarray(p[:, b].transpose(0, 2, 1))
        in_maps.append(m)
    res = run_bass_kernel_spmd(nc, in_maps, core_ids=list(range(B)))
    out = np.stack([np.ascontiguousarray(r["yT"].T) for r in res.results], axis=0)
    return out.astype(np.float32)
```
